# Optimizing a Trainium2 kernel written in Bass

```python
import math
import jax, jax.numpy as jnp
from jax import lax
import numpy as np

D_MODEL = 1024
BATCH = 8
SEQ = 2048
DEPTH = 1

SSM_EXPAND = 2
SSM_D_INNER = SSM_EXPAND * D_MODEL
SSM_HEAD_DIM = 64
SSM_N_HEADS = SSM_D_INNER // SSM_HEAD_DIM
SSM_N_GROUPS = 4
SSM_HEADS_PER_GROUP = SSM_N_HEADS // SSM_N_GROUPS
SSM_D_STATE = 128
SSM_CONV = 4
SSM_CHUNK = 128
SSM_CONV_DIM = SSM_D_INNER + 2 * SSM_N_GROUPS * SSM_D_STATE

ATT_HEAD_DIM = 64
ATT_N_HEADS = D_MODEL // (2 * ATT_HEAD_DIM)
ATT_V_DIM = 2 * ATT_HEAD_DIM
ATT_Q_BLOCK = 128
ROPE_THETA = 500000.0
ROPE_DIM = ATT_HEAD_DIM // 4

FFN_HIDDEN = ((8 * D_MODEL + 3 * 256 - 1) // (3 * 256)) * 256

RMS_EPS = 1e-6

Z_END = SSM_D_INNER
XBC_END = Z_END + SSM_CONV_DIM
DT_END = XBC_END + SSM_N_HEADS
Q_END = DT_END + 2 * ATT_N_HEADS * ATT_HEAD_DIM
K_END = Q_END + 2 * ATT_N_HEADS * ATT_HEAD_DIM
V_END = K_END + ATT_N_HEADS * ATT_V_DIM
IN_COLS = V_END + 2 * D_MODEL

kernel_name = "hybrid_ssd_diffattn_gated_block"


def rmsnorm(x, w):
    xf = x.astype(jnp.float32)
    y = xf * lax.rsqrt(jnp.mean(xf * xf, axis=-1, keepdims=True) + RMS_EPS)
    return (y * w.astype(jnp.float32)).astype(x.dtype)


def rope_partial(x, positions):
    half = ROPE_DIM // 2
    inv_freq = ROPE_THETA ** (-jnp.arange(0, ROPE_DIM, 2, dtype=jnp.float32) / ROPE_DIM)
    ang = positions.astype(jnp.float32)[..., None] * inv_freq
    cos = jnp.cos(ang)[:, :, None, :]
    sin = jnp.sin(ang)[:, :, None, :]
    x1 = x[..., :half].astype(jnp.float32)
    x2 = x[..., half:ROPE_DIM].astype(jnp.float32)
    rot = jnp.concatenate([x1 * cos - x2 * sin, x2 * cos + x1 * sin], axis=-1)
    return jnp.concatenate([rot.astype(x.dtype), x[..., ROPE_DIM:]], axis=-1)


def causal_depthwise_conv(u, w, b):
    out = lax.conv_general_dilated(
        u, w[:, None, :].astype(u.dtype), window_strides=(1,), padding=[(SSM_CONV - 1, 0)],
        dimension_numbers=("NWC", "WIO", "NWC"), feature_group_count=u.shape[-1])
    return out + b


def ssd_chunked(xh, dt, A, Bm, Cm):
    b, S = xh.shape[0], xh.shape[1]
    nc = S // SSM_CHUNK
    G, R, P, N = SSM_N_GROUPS, SSM_HEADS_PER_GROUP, SSM_HEAD_DIM, SSM_D_STATE
    xdt = (xh * dt[..., None]).reshape(b, nc, SSM_CHUNK, G, R, P)
    dA = (dt * A).reshape(b, nc, SSM_CHUNK, G, R)
    Bc = Bm.reshape(b, nc, SSM_CHUNK, G, N)
    Cc = Cm.reshape(b, nc, SSM_CHUNK, G, N)
    a_cs = jnp.cumsum(dA, axis=2)
    causal = jnp.tril(jnp.ones((SSM_CHUNK, SSM_CHUNK), dtype=bool))
    seg = a_cs[:, :, :, None] - a_cs[:, :, None, :]
    decay = jnp.exp(jnp.where(causal[None, None, :, :, None, None], seg, -jnp.inf))
    cb = jnp.einsum("bclgn,bcsgn->bclsg", Cc, Bc)
    y_diag = jnp.einsum("bclsg,bclsgr,bcsgrp->bclgrp", cb, decay, xdt)
    decay_to_end = jnp.exp(a_cs[:, :, -1:] - a_cs)
    chunk_states = jnp.einsum("bclgn,bclgr,bclgrp->bcgrpn", Bc, decay_to_end, xdt)
    chunk_decay = jnp.exp(a_cs[:, :, -1])

    def step(h, inp):
        st, dec = inp
        return h * dec[..., None, None] + st, h

    h0 = jnp.zeros((b, G, R, P, N), dtype=chunk_states.dtype)
    _, h_in = lax.scan(step, h0, (jnp.moveaxis(chunk_states, 1, 0), jnp.moveaxis(chunk_decay, 1, 0)))
    h_in = jnp.moveaxis(h_in, 0, 1)
    y_off = jnp.einsum("bclgn,bcgrpn,bclgr->bclgrp", Cc, h_in, jnp.exp(a_cs))
    return (y_diag + y_off).reshape(b, S, SSM_N_HEADS, P)


def mamba2_mixer(z, xbc, dt_raw, conv_w, conv_b, dt_bias, a_log, d_skip, norm_w):
    b, S, _ = z.shape
    xbc = jax.nn.silu(causal_depthwise_conv(xbc, conv_w, conv_b))
    xs, Bm, Cm = jnp.split(xbc, [SSM_D_INNER, SSM_D_INNER + SSM_N_GROUPS * SSM_D_STATE], axis=-1)
    xh = xs.reshape(b, S, SSM_N_HEADS, SSM_HEAD_DIM)
    Bm = Bm.reshape(b, S, SSM_N_GROUPS, SSM_D_STATE)
    Cm = Cm.reshape(b, S, SSM_N_GROUPS, SSM_D_STATE)
    dt = jax.nn.softplus((dt_raw + dt_bias).astype(jnp.float32))
    A = -jnp.exp(a_log.astype(jnp.float32))
    y = ssd_chunked(xh, dt, A, Bm, Cm) + xh * d_skip[:, None]
    y = y.reshape(b, S, SSM_D_INNER) * jax.nn.silu(z)
    g = y.reshape(b, S, SSM_N_GROUPS, SSM_D_INNER // SSM_N_GROUPS).astype(jnp.float32)
    g = g * lax.rsqrt(jnp.mean(g * g, axis=-1, keepdims=True) + RMS_EPS)
    return (g.reshape(b, S, SSM_D_INNER) * norm_w.astype(jnp.float32)).astype(z.dtype)


def diff_attention(q, k, v, positions, lam_q1, lam_k1, lam_q2, lam_k2, subln_w, lam_init):
    b, S, _ = q.shape
    H, d = ATT_N_HEADS, ATT_HEAD_DIM
    q = rope_partial(q.reshape(b, S, 2 * H, d), positions).reshape(b, S, H, 2, d)
    k = rope_partial(k.reshape(b, S, 2 * H, d), positions).reshape(b, S, H, 2, d)
    v = v.reshape(b, S, H, ATT_V_DIM)
    lam = (jnp.exp(jnp.sum(lam_q1.astype(jnp.float32) * lam_k1.astype(jnp.float32)))
           - jnp.exp(jnp.sum(lam_q2.astype(jnp.float32) * lam_k2.astype(jnp.float32))) + lam_init)
    nb = S // ATT_Q_BLOCK
    qb = (q * (d ** -0.5)).reshape(b, nb, ATT_Q_BLOCK, H, 2, d).transpose(1, 0, 2, 3, 4, 5)
    key_pos = jnp.arange(S)

    def block(args):
        qi, i = args
        s = jnp.einsum("bqhmd,bkhmd->bhmqk", qi, k).astype(jnp.float32)
        q_pos = i * ATT_Q_BLOCK + jnp.arange(ATT_Q_BLOCK)
        mask = key_pos[None, :] <= q_pos[:, None]
        p = jax.nn.softmax(jnp.where(mask, s, -jnp.inf), axis=-1)
        w = p[:, :, 0] - lam * p[:, :, 1]
        return jnp.einsum("bhqk,bkhe->bqhe", w.astype(v.dtype), v)

    o = lax.map(block, (qb, jnp.arange(nb)))
    o = o.transpose(1, 0, 2, 3, 4).reshape(b, S, H, ATT_V_DIM)
    o = rmsnorm(o, subln_w) * (1.0 - lam_init)
    return o.reshape(b, S, H * ATT_V_DIM)


def setup_inputs(seed: int = 0) -> dict:
    key = jax.random.key(seed)
    ks = jax.random.split(key, 24)
    f32 = jnp.float32

    def nrm(k, shape, scale):
        return jax.random.normal(k, shape, f32) * scale

    def gain(k, n):
        return 1.0 + 0.02 * jax.random.normal(k, (DEPTH, n), f32)

    x = jax.random.normal(ks[0], (BATCH, SEQ, D_MODEL), f32)
    offsets = jax.random.randint(ks[1], (BATCH, 1), 0, 4096, dtype=jnp.int32)
    positions = jnp.arange(SEQ, dtype=jnp.int32)[None, :] + offsets
    u = jax.random.uniform(ks[5], (DEPTH, SSM_N_HEADS), f32)
    dt0 = jnp.exp(u * (math.log(0.1) - math.log(0.001)) + math.log(0.001))
    dt_bias = dt0 + jnp.log(-jnp.expm1(-dt0))
    a_log = jnp.log(jax.random.uniform(ks[6], (DEPTH, SSM_N_HEADS), f32, 1.0, 16.0))
    return {
        "x": x,
        "positions": positions,
        "w_in": nrm(ks[2], (DEPTH, D_MODEL, IN_COLS), D_MODEL ** -0.5),
        "conv_w": nrm(ks[3], (DEPTH, SSM_CONV, SSM_CONV_DIM), SSM_CONV ** -0.5),
        "conv_b": nrm(ks[4], (DEPTH, SSM_CONV_DIM), 0.02),
        "dt_bias": dt_bias,
        "a_log": a_log,
        "d_skip": 1.0 + 0.02 * jax.random.normal(ks[7], (DEPTH, SSM_N_HEADS), f32),
        "ssm_norm_w": gain(ks[8], SSM_D_INNER),
        "w_ssm_out": nrm(ks[9], (DEPTH, SSM_D_INNER, D_MODEL), SSM_D_INNER ** -0.5),
        "lam_q1": nrm(ks[10], (DEPTH, ATT_HEAD_DIM), 0.1),
        "lam_k1": nrm(ks[11], (DEPTH, ATT_HEAD_DIM), 0.1),
        "lam_q2": nrm(ks[12], (DEPTH, ATT_HEAD_DIM), 0.1),
        "lam_k2": nrm(ks[13], (DEPTH, ATT_HEAD_DIM), 0.1),
        "attn_subln_w": gain(ks[14], ATT_V_DIM),
        "w_attn_out": nrm(ks[15], (DEPTH, ATT_N_HEADS * ATT_V_DIM, D_MODEL), (ATT_N_HEADS * ATT_V_DIM) ** -0.5),
        "w_mix_out": nrm(ks[16], (DEPTH, D_MODEL, D_MODEL), D_MODEL ** -0.5),
        "norm_pre_mix": gain(ks[17], D_MODEL),
        "norm_post_mix": gain(ks[18], D_MODEL),
        "norm_pre_ffn": gain(ks[19], D_MODEL),
        "norm_post_ffn": gain(ks[20], D_MODEL),
        "w_ffn_gate": nrm(ks[21], (DEPTH, D_MODEL, FFN_HIDDEN), D_MODEL ** -0.5),
        "w_ffn_up": nrm(ks[22], (DEPTH, D_MODEL, FFN_HIDDEN), D_MODEL ** -0.5),
        "w_ffn_down": nrm(ks[23], (DEPTH, FFN_HIDDEN, D_MODEL), FFN_HIDDEN ** -0.5),
    }


def reference(x, positions, w_in, conv_w, conv_b, dt_bias, a_log, d_skip, ssm_norm_w, w_ssm_out,
              lam_q1, lam_k1, lam_q2, lam_k2, attn_subln_w, w_attn_out, w_mix_out,
              norm_pre_mix, norm_post_mix, norm_pre_ffn, norm_post_ffn,
              w_ffn_gate, w_ffn_up, w_ffn_down):
    for l in range(DEPTH):
        lam_init = 0.8 - 0.6 * math.exp(-0.3 * l)
        h = rmsnorm(x, norm_pre_mix[l])
        proj = h @ w_in[l]
        z, xbc, dt_raw, q, k, v, gate_logits = jnp.split(
            proj, [Z_END, XBC_END, DT_END, Q_END, K_END, V_END], axis=-1)
        y_ssm = mamba2_mixer(z, xbc, dt_raw, conv_w[l], conv_b[l], dt_bias[l], a_log[l],
                             d_skip[l], ssm_norm_w[l]) @ w_ssm_out[l]
        y_att = diff_attention(q, k, v, positions, lam_q1[l], lam_k1[l], lam_q2[l], lam_k2[l],
                               attn_subln_w[l], lam_init) @ w_attn_out[l]
        gates = jax.nn.sigmoid(gate_logits.astype(jnp.float32)).astype(x.dtype)
        g_ssm, g_att = jnp.split(gates, 2, axis=-1)
        mixed = (g_ssm * y_ssm + g_att * y_att) @ w_mix_out[l]
        x = x + rmsnorm(mixed, norm_post_mix[l])
        h = rmsnorm(x, norm_pre_ffn[l])
        f = (jax.nn.silu(h @ w_ffn_gate[l]) * (h @ w_ffn_up[l])) @ w_ffn_down[l]
        x = x + rmsnorm(f, norm_post_ffn[l])
    return x
```

```python
import math
import numpy as np
from contextlib import ExitStack
import concourse.bass as bass
import concourse.mybir as mybir
from concourse.bass_utils import run_bass_kernel_spmd

F32 = mybir.dt.float32
BF16 = mybir.dt.bfloat16
I32 = mybir.dt.int32
ALU = mybir.AluOpType
AF = mybir.ActivationFunctionType
AX = mybir.AxisListType

D = 1024
S = 2048
TH = 1024
NHALF = 2
FH = 2816
NHC = 22
Z0, X0, DT0, Q0, K0, V0, G0 = 0, 2048, 5120, 5152, 6176, 7200, 8224
EPS = 1e-6
LAM_INIT = 0.8 - 0.6 * math.exp(-0.0)
WSLOT = 5120
NWS = 3
ARENA = 28 * 1024
GRAN = 256


class Tok:
    __slots__ = ("name", "w", "r", "excl")

    def __init__(self, name="", excl=False):
        self.name = name
        self.w = None
        self.r = []
        self.excl = excl


class Op:
    __slots__ = ("eng", "fn", "deps", "sig", "tick", "dma", "dsem", "dcnt")


def _flat(x):
    out = []
    if isinstance(x, Tok):
        return [x]
    for t in x:
        if isinstance(t, (list, tuple)):
            out.extend(_flat(t))
        elif t is not None:
            out.append(t)
    return out


class Prog:
    ENGS = ("pe", "act", "dve", "pool", "sp")
    ND = 8

    def __init__(self, nc, es):
        self.nc = nc
        self.es = es
        self.by_eng = {e: [] for e in self.ENGS}

    def tok(self, name=""):
        return Tok(name)

    def add(self, eng, fn, reads=(), writes=(), dma=False):
        reads = _flat(reads)
        writes = _flat(writes)
        op = Op()
        op.eng = eng
        op.fn = fn
        op.dma = dma
        op.sig = False
        op.tick = 0
        deps = set()
        for t in reads:
            if t.w is not None:
                deps.add(t.w)
            if t.excl:
                deps.update(r_ for r_ in t.r if r_.eng != eng)
        for t in writes:
            if t.w is not None:
                deps.add(t.w)
            deps.update(t.r)
        for t in reads:
            t.r.append(op)
        for t in writes:
            t.w = op
            t.r = []
        deps.discard(op)
        if eng == "pe" and not dma:
            deps = {d for d in deps if d.dma or d.eng != "pe"}
        for d in deps:
            if not d.dma:
                d.sig = True
        op.deps = deps
        self.by_eng[eng].append(op)
        return op

    def dma(self, eng, out, in_, reads=(), writes=()):
        return self.add(eng, lambda e: e.dma_start(out=out, in_=in_), reads, writes, dma=True)

    def emit(self):
        nc, es = self.nc, self.es
        sem = {e: es.enter_context(nc.semaphore(f"s_{e}")) for e in self.ENGS}
        rings = {}
        for e in self.ENGS:
            cnt = 0
            k = 0
            ring = None
            counts = None
            for op in self.by_eng[e]:
                if op.dma:
                    if ring is None:
                        ring = [es.enter_context(nc.semaphore(f"d_{e}{i}")) for i in range(self.ND)]
                        counts = [0] * self.ND
                        rings[e] = (ring, counts)
                    i = k % self.ND
                    k += 1
                    counts[i] += 16
                    op.dsem = ring[i]
                    op.dcnt = counts[i]
                elif op.sig:
                    cnt += 1
                    op.tick = cnt
        self.stats = {e: len(v) for e, v in self.by_eng.items()}
        nwaits = {e: 0 for e in self.ENGS}

        def run(e, eng):
            seen = {}

            def wait(s, v):
                if seen.get(id(s), 0) >= v:
                    return
                seen[id(s)] = v
                eng.wait_ge(s, v)
                nwaits[e] += 1

            for op in self.by_eng[e]:
                for d in op.deps:
                    if d.dma:
                        wait(d.dsem, d.dcnt)
                    else:
                        wait(sem[d.eng], d.tick)
                if op.dma:
                    if op.dcnt > 16:
                        wait(op.dsem, op.dcnt - 16)
                    op.fn(eng).then_inc(op.dsem, 16)
                else:
                    ins = op.fn(eng)
                    if op.sig:
                        ins.then_inc(sem[e], 1)
            if e in rings:
                ring, counts = rings[e]
                for s, c in zip(ring, counts):
                    if c:
                        wait(s, c)

        with nc.Block() as block:
            @block.tensor
            def _(eng):
                run("pe", eng)

            @block.scalar
            def _(eng):
                run("act", eng)

            @block.vector
            def _(eng):
                run("dve", eng)

            @block.gpsimd
            def _(eng):
                run("pool", eng)

            @block.sync
            def _(eng):
                run("sp", eng)
        self.nwaits = nwaits


class _Stop(Exception):
    pass


class Buf:
    __slots__ = ("ap", "t")

    def __init__(self, ap, t):
        self.ap = ap
        self.t = t


def build(upto="F", dbg=()):
    nc = bass.Bass("TRN2", target_bir_lowering=False)

    def din(name, shape, dt=F32):
        return nc.dram_tensor(name, list(shape), dt, kind="ExternalInput").ap()

    x_d = din("x", [S, D])
    pos_d = din("posr", [128, 16], I32)
    wxbc_d = din("w_xbc", [128, 24, 8, 128])
    wz_d = din("w_z", [128, 4, 8, 512])
    wdt_d = din("w_dt", [128, 8, 32])
    wqkv_d = din("w_qkv", [128, 8, 8, 384])
    wc_d = din("w_c", [128, 8, 40, 128])
    wmix_d = din("w_mix", [128, 2, 8, 512])
    wgu_d = din("w_gu", [128, NHC, 16, 128])
    wd_d = din("w_d", [128, 8, NHC, 128])
    NV = 8 + 8 + 16 + 24 + 96 + 1
    vecs_d = din("vecs", [128, NV])
    NR = 32 * 3 + 64 * 4
    rows_d = din("rows", [NR])
    g24_d = din("g24", [2, D])
    out_d = nc.dram_tensor("out", [S, D], F32, kind="ExternalOutput").ap()
    dbg_d = {}
    for name, shape in dbg:
        dbg_d[name] = nc.dram_tensor("dbg_" + name, list(shape), F32, kind="ExternalOutput").ap()

    es = ExitStack()
    with es:
        P = Prog(nc, es)

        def sbt(name, shape, dt):
            return es.enter_context(nc.sbuf_tensor("sb_" + name, list(shape), dt))

        def pbuf(name, shape, dt, ntok=1):
            t = sbt(name, shape, dt)
            return Buf(t, [P.tok(name + str(i)) for i in range(ntok)])

        def MM(out, lhsT, rhs, start, stop, r, w):
            P.add("pe", lambda e: e.matmul(out, lhsT, rhs, start=start, stop=stop), r, w)

        def TR(out, in_, ident, r, w):
            P.add("pe", lambda e: e.transpose(out=out, in_=in_, identity=ident), r, w)

        def ACT(out, in_, func, r, w, bias=None, scale=None, accum=None):
            kw = {}
            if bias is not None:
                kw["bias"] = bias
            if scale is not None:
                kw["scale"] = scale
            if accum is not None:
                kw["accum_out"] = accum
            P.add("act", lambda e: e.activation(out=out, in_=in_, func=func, **kw), r, w)

        def TS(out, in0, s1, s2, op0, op1, r, w, eng="dve"):
            if op1 is None:
                P.add(eng, lambda e: e.tensor_scalar(out=out, in0=in0, scalar1=s1, scalar2=None, op0=op0), r, w)
            else:
                P.add(eng, lambda e: e.tensor_scalar(out=out, in0=in0, scalar1=s1, scalar2=s2, op0=op0, op1=op1), r, w)

        def TT(out, in0, in1, op, r, w, eng="dve"):
            P.add(eng, lambda e: e.tensor_tensor(out=out, in0=in0, in1=in1, op=op), r, w)

        def STT(out, in0, scalar, in1, op0, op1, r, w):
            P.add("dve", lambda e: e.scalar_tensor_tensor(out=out, in0=in0, scalar=scalar, in1=in1, op0=op0, op1=op1), r, w)

        def CP(out, in_, r, w, eng="dve", raw=False):
            if eng == "act":
                P.add("act", lambda e: e.copy(out=out, in_=in_), r, w)
            elif raw:
                P.add(eng, lambda e: e.tensor_copy(out=out, in_=in_), r, w)
            else:
                P.add(eng, lambda e: e.tensor_scalar(out=out, in0=in_, scalar1=1.0, scalar2=None, op0=ALU.mult), r, w)

        def gtok(buf, lo, hi):
            return buf.t[lo // (2 * GRAN):(hi + 2 * GRAN - 1) // (2 * GRAN)]

        def _stt_acc(out, in_, acc):
            return lambda e: e.scalar_tensor_tensor(out=out, in0=in_, scalar=1.0, in1=in_, op0=ALU.mult, op1=ALU.mult, accum_out=acc)

        def TRED(out, in_, r, w):
            P.add("dve", lambda e: e.tensor_reduce(out=out, in_=in_, axis=AX.X, op=ALU.add), r, w)

        def RCP(out, in_, r, w):
            P.add("dve", lambda e: e.reciprocal(out=out, in_=in_), r, w)

        def MSET(ap, val, w, eng="dve"):
            P.add(eng, lambda e: e.memset(ap, val), (), w)

        dbg_tok = P.tok("dbg")

        def DUMP(name, ap, r, rows=None):
            if name in dbg_d:
                dst = dbg_d[name] if rows is None else dbg_d[name][rows]
                P.dma("pool", dst, ap, r, [dbg_tok])

        ps = es.enter_context(nc.psum_tensor("ps", [128, 8, 512], F32))
        PB = [ps[:, i, :] for i in range(8)]
        PBb = [ps[:, i, :].bitcast(BF16) for i in range(8)]
        PT = [Tok(f"bank{i}", excl=True) for i in range(8)]

        arena_t = sbt("arena", [128, ARENA], BF16)
        arena_tok = [P.tok(f"ar{i}") for i in range(ARENA // GRAN)]
        ar_off = [0]

        def ar_reset(off=0):
            ar_off[0] = off

        def ar(shape, dt):
            n = 1
            for s_ in shape:
                n *= s_
            nb = n * (2 if dt == F32 or dt == I32 else 1)
            off = ar_off[0]
            nb_al = ((nb + GRAN - 1) // GRAN) * GRAN
            assert off + nb_al <= ARENA, ("arena overflow", off, nb_al)
            ar_off[0] = off + nb_al
            ap = arena_t[:, off:off + nb]
            if dt != BF16:
                ap = ap.bitcast(dt)
            if len(shape) == 2:
                ap = ap.rearrange("p (a b) -> p a b", a=shape[0], b=shape[1])
            elif len(shape) == 3:
                ap = ap.rearrange("p (a b c) -> p a b c", a=shape[0], b=shape[1], c=shape[2])
            return Buf(ap, arena_tok[off // GRAN:(off + nb_al) // GRAN])

        wts_t = sbt("wts", [128, NWS, WSLOT], BF16)
        wts_tok = [P.tok(f"ws{i}") for i in range(NWS)]
        wctr = [0]

        def wload(src, shape):
            i = wctr[0] % NWS
            wctr[0] += 1
            n = 1
            for s_ in shape:
                n *= s_
            assert n <= WSLOT
            ap = wts_t[:, i, 0:n]
            if len(shape) == 2:
                ap = ap.rearrange("p (a b) -> p a b", a=shape[0], b=shape[1])
            elif len(shape) == 3:
                ap = ap.rearrange("p (a b c) -> p a b c", a=shape[0], b=shape[1], c=shape[2])
            P.dma("pool", ap, src, (), [wts_tok[i]])
            return Buf(ap, [wts_tok[i]])

        ident_b = pbuf("ident_b", [128, 128], BF16)
        ident_f = pbuf("ident_f", [128, 128], F32)
        tri_f = pbuf("tri_f", [128, 128], F32)
        tri_b = pbuf("tri_b", [128, 128], BF16)
        ugt_f = pbuf("ugt_f", [128, 128], F32)
        ones_f = pbuf("ones_f", [128, 128], F32)
        vecs = pbuf("vecs", [128, NV], F32)
        rows = pbuf("rows", [128, NR], F32)
        epsb = pbuf("epsb", [128, 1], F32)
        oneb = pbuf("oneb", [128, 1], F32)
        wdt = pbuf("wdt", [128, 8, 32], BF16)
        A_bc = pbuf("A_bc", [128, 32], F32)
        neglam = pbuf("neglam", [128, 1], F32)
        swl = pbuf("swl", [128, 1], F32)
        sc = pbuf("sc", [128, 16, 16], F32)
        Sst = pbuf("Sst", [128, 4, 512], F32, 4)
        Hin = pbuf("Hin", [128, 4, 512], BF16, 4)
        tails = pbuf("tails", [128, 24, 3], BF16, 24)
        dtb = {k: pbuf("dt_" + k, [128, 8, 32], F32) for k in ("dt", "dA", "eacs", "dte", "cd", "dtdte")}
        kcache = pbuf("kcache", [128, 8, TH], BF16, 64)
        vcache = pbuf("vcache", [128, 8, 8, 130], BF16, 64)
        hT = pbuf("hT", [128, 8, TH], BF16, 2)
        big = pbuf("big", [128, 24, TH], BF16, 48)

        g1 = vecs.ap[:, 0:8]
        g3 = vecs.ap[:, 8:16]
        nw = vecs.ap[:, 16:32]
        cb = vecs.ap[:, 32:56]
        cw = vecs.ap[:, 56:152].rearrange("p (c k) -> p c k", c=24, k=4)
        sublnw = vecs.ap[:, 152:153]
        dtb_bc = rows.ap[:, 0:32]
        alog_bc = rows.ap[:, 32:64]
        D_bc = rows.ap[:, 64:96]
        lamv = rows.ap[:, 96:352]

        def yT_t(kc, tt):
            return big.t[kc * 2 + tt]

        P.dma("sp", vecs.ap[:], vecs_d, (), vecs.t)
        P.dma("sp", rows.ap[:], rows_d.partition_broadcast(128), (), rows.t)
        P.dma("pool", wdt.ap[:], wdt_d, (), wdt.t)
        MSET(epsb.ap[:], EPS, epsb.t)
        MSET(oneb.ap[:], 1.0, oneb.t)
        MSET(ones_f.ap[:], 1.0, ones_f.t)
        MSET(tails.ap[:], 0.0, tails.t)
        MSET(Sst.ap[:], 0.0, Sst.t)
        MSET(Hin.ap[:], 0.0, Hin.t)
        MSET(vcache.ap[:], 1.0, vcache.t)

        def mask(buf, pattern, base, cm, op):
            MSET(buf.ap[:], 1.0, buf.t, eng="pool")
            P.add("pool", lambda e: e.affine_select(out=buf.ap[:], in_=buf.ap[:], pattern=pattern, compare_op=op,
                                                    fill=0.0, base=base, channel_multiplier=cm), buf.t, buf.t)

        mask(ident_b, [[-1, 128]], 0, 1, ALU.is_equal)
        mask(ident_f, [[-1, 128]], 0, 1, ALU.is_equal)
        mask(tri_f, [[1, 128]], 0, -1, ALU.is_ge)
        mask(tri_b, [[1, 128]], 0, -1, ALU.is_ge)
        mask(ugt_f, [[-1, 128]], -1, 1, ALU.is_ge)

        ar_reset()
        ACT(A_bc.ap[:], alog_bc, AF.Exp, rows.t, A_bc.t)
        TS(A_bc.ap[:], A_bc.ap[:], -1.0, None, ALU.mult, None, A_bc.t, A_bc.t)
        lt = ar([256], F32)
        l2 = ar([2], F32)
        TT(lt.ap[:, 0:64], lamv[:, 0:64], lamv[:, 64:128], ALU.mult, rows.t, lt.t)
        TT(lt.ap[:, 64:128], lamv[:, 128:192], lamv[:, 192:256], ALU.mult, rows.t, lt.t)
        P.add("dve", lambda e: e.tensor_reduce(out=l2.ap[:, 0:2], in_=lt.ap[:, 0:128].rearrange("p (a b) -> p a b", a=2, b=64),
                                               axis=AX.X, op=ALU.add), lt.t, l2.t)
        ACT(l2.ap[:, 0:2], l2.ap[:, 0:2], AF.Exp, l2.t, l2.t)
        TT(neglam.ap[:], l2.ap[:, 1:2], l2.ap[:, 0:1], ALU.subtract, l2.t, neglam.t)
        TS(neglam.ap[:], neglam.ap[:], -LAM_INIT, None, ALU.add, None, neglam.t, neglam.t)
        TS(swl.ap[:], sublnw, 1.0 - LAM_INIT, None, ALU.mult, None, vecs.t, swl.t)
        posi = ar([16], I32)
        posf = ar([16], F32)
        invf = ar([16], F32)
        ang = ar([16, 16], F32)
        uu = ar([16, 16], F32)
        ki = ar([16, 16], I32)
        kf = ar([16, 16], F32)
        mk = ar([16, 16], F32)
        P.dma("sp", posi.ap[:], pos_d, (), posi.t)
        CP(posf.ap[:], posi.ap[:], posi.t, posf.t, raw=True)
        for i in range(8):
            f = float(500000.0 ** (-(2.0 * i) / 16.0))
            MSET(invf.ap[:, i:i + 1], f, invf.t)
            MSET(invf.ap[:, 8 + i:9 + i], f, invf.t)
        TT(ang.ap[:], posf.ap[:].unsqueeze(2).broadcast_to([128, 16, 16]),
           invf.ap[:].unsqueeze(1).broadcast_to([128, 16, 16]), ALU.mult, [posf.t, invf.t], ang.t)
        TS(ang.ap[:, :, 8:16], ang.ap[:, :, 8:16], math.pi / 2, None, ALU.add, None, ang.t, ang.t)
        TS(uu.ap[:], ang.ap[:], 1.0 / (2 * math.pi), None, ALU.mult, None, ang.t, uu.t)
        CP(ki.ap[:], uu.ap[:], uu.t, ki.t, raw=True)
        CP(kf.ap[:], ki.ap[:], ki.t, kf.t, raw=True)
        C1 = 6.28125
        C2 = 2 * math.pi - C1
        STT(ang.ap[:], kf.ap[:], -C1, ang.ap[:], ALU.mult, ALU.add, [kf.t, ang.t], ang.t)
        STT(ang.ap[:], kf.ap[:], -C2, ang.ap[:], ALU.mult, ALU.add, [kf.t, ang.t], ang.t)
        TS(mk.ap[:], ang.ap[:], math.pi, None, ALU.is_gt, None, ang.t, mk.t)
        STT(ang.ap[:], mk.ap[:], -2 * math.pi, ang.ap[:], ALU.mult, ALU.add, [mk.t, ang.t], ang.t)
        TS(mk.ap[:], ang.ap[:], -math.pi, None, ALU.is_lt, None, ang.t, mk.t)
        STT(ang.ap[:], mk.ap[:], 2 * math.pi, ang.ap[:], ALU.mult, ALU.add, [mk.t, ang.t], ang.t)
        TS(ang.ap[:], ang.ap[:], math.pi, -math.pi, ALU.min, ALU.max, ang.t, ang.t)
        ACT(sc.ap[:], ang.ap[:], AF.Sin, ang.t, sc.t)

        def rms_rstd(src_ap, n, r, junk, ss, rstd):
            ACT(junk.ap, src_ap, AF.Square, r, [junk.t, ss.t], accum=ss.ap)
            ACT(rstd.ap, ss.ap, AF.Ln, [ss.t, epsb.t], rstd.t, bias=epsb.ap[:], scale=1.0 / n)
            ACT(rstd.ap, rstd.ap, AF.Exp, rstd.t, rstd.t, scale=-0.5)

        def to_hT(xs, b, gvec, bank):
            pv = PBb[bank].rearrange("p (a b) -> p a b", a=8, b=128)
            for kc in range(8):
                TR(pv[:, kc, :], xs.ap[:, kc * 128:(kc + 1) * 128], ident_b.ap[:], [xs.t, ident_b.t], PT[bank])
            TT(hT.ap[:, :, b * 128:(b + 1) * 128], pv, gvec.unsqueeze(2).broadcast_to([128, 8, 128]), ALU.mult,
               [PT[bank], vecs.t], hT.t[b // 4])

        try:
            for hf in range(NHALF):
                t0 = hf * TH
                ar_reset()
                xb = [ar([D], F32) for _ in range(2)]
                xs = [ar([D], BF16) for _ in range(2)]
                junk = ar([D], F32)
                ss = [ar([1], F32) for _ in range(2)]
                rstd = [ar([1], F32) for _ in range(2)]
                def z_chain(b):
                    i = b % 2
                    P.dma("sp", xb[i].ap, x_d[t0 + b * 128:t0 + (b + 1) * 128, :], (), xb[i].t)
                    rms_rstd(xb[i].ap, D, xb[i].t, junk, ss[i], rstd[i])
                    TS(xs[i].ap, xb[i].ap, rstd[i].ap, None, ALU.mult, None, [xb[i].t, rstd[i].t], xs[i].t)

                z_chain(0)
                for b in range(8):
                    if b + 1 < 8:
                        z_chain(b + 1)
                    to_hT(xs[b % 2], b, g1, b % 2)
                if hf == 0:
                    DUMP("hT", hT.ap[:, :, 0:128], hT.t)
                if upto == "0":
                    continue

                ar_reset()
                for b in range(8):
                    for kc in range(8):
                        MM(PB[2][:, b * 32:(b + 1) * 32], hT.ap[:, kc, b * 128:(b + 1) * 128], wdt.ap[:, kc, :], kc == 0, kc == 7,
                           [hT.t[b // 4], wdt.t], PT[2])
                v3 = lambda ap: ap.rearrange("p (a b) -> p a b", a=8, b=32)
                xd = ar([8, 32], F32)
                ta = ar([8, 32], F32)
                dt_, dA, eacs, dte, cd, dtdte = (dtb[k] for k in ("dt", "dA", "eacs", "dte", "cd", "dtdte"))
                TT(xd.ap, v3(PB[2][:, 0:256]), dtb_bc.unsqueeze(1).broadcast_to([128, 8, 32]), ALU.add, [PT[2], rows.t], xd.t)
                TS(ta.ap, xd.ap, -1.0, None, ALU.mult, None, xd.t, ta.t)
                TT(ta.ap, ta.ap, xd.ap, ALU.min, [ta.t, xd.t], ta.t)
                ACT(ta.ap, ta.ap, AF.Exp, ta.t, ta.t)
                ACT(ta.ap, ta.ap, AF.Ln, [ta.t, oneb.t], ta.t, bias=oneb.ap[:])
                TS(xd.ap, xd.ap, 0.0, None, ALU.max, None, xd.t, xd.t)
                TT(dt_.ap[:], xd.ap, ta.ap, ALU.add, [xd.t, ta.t], dt_.t)
                TT(dA.ap[:], dt_.ap[:], A_bc.ap[:].unsqueeze(1).broadcast_to([128, 8, 32]), ALU.mult, [dt_.t, A_bc.t], dA.t)
                for c in range(8):
                    for bank, m in ((3, tri_f), (4, ugt_f), (5, ones_f)):
                        MM(PB[bank][:, c * 32:(c + 1) * 32], m.ap[:], dA.ap[:, c, :], True, True, [m.t, dA.t], PT[bank])
                ACT(eacs.ap[:], v3(PB[3][:, 0:256]), AF.Exp, PT[3], eacs.t)
                ACT(dte.ap[:], v3(PB[4][:, 0:256]), AF.Exp, PT[4], dte.t)
                ACT(cd.ap[:], v3(PB[5][:, 0:256]), AF.Exp, PT[5], cd.t)
                TT(dtdte.ap[:], dt_.ap[:], dte.ap[:], ALU.mult, [dt_.t, dte.t], dtdte.t)
                if hf == 0:
                    DUMP("dt", dt_.ap[:, 0, :], dt_.t)
                if upto == "S1":
                    raise _Stop

                xsT = ar([4, TH], BF16)
                BT = ar([TH], BF16)
                CT = ar([TH], BF16)
                sz_all = ar([8, 512], BF16)
                yg_all = ar([8, 512], BF16)
                ssg_all = ar([8], F32)
                rsg_all = ar([8], F32)
                off_s = ar_off[0]
                v8 = lambda ap: ap.rearrange("p (a b) -> p a b", a=8, b=64)

                def s_loadw(g):
                    return (wload(wxbc_d[:, 6 * g:6 * g + 4], [4, 8, 128]),
                            wload(wxbc_d[:, 6 * g + 4:6 * g + 6], [2, 8, 128]),
                            wload(wz_d[:, g], [8, 512]))

                wnext = s_loadw(0)
                for g in range(4):
                    ar_reset(off_s)
                    wx, wbc, wz = wnext
                    hs = slice(8 * g, 8 * g + 8)
                    upad = [ar([515], BF16) for _ in range(2)]
                    dg = ar([6, 4, 128], BF16)
                    TT(dg.ap, ident_b.ap[:].unsqueeze(1).unsqueeze(1).broadcast_to([128, 6, 4, 128]),
                       cw[:, 6 * g:6 * g + 6, :].unsqueeze(3).broadcast_to([128, 6, 4, 128]), ALU.mult, [ident_b.t, vecs.t], dg.t)
                    tiles = [(j, tt) for j in range(6) for tt in range(2)]

                    def p1_mm(n):
                        j, tt = tiles[n]
                        i = n % 2
                        wsrc = (wx, j) if j < 4 else (wbc, j - 4)
                        cols = slice(tt * 512, (tt + 1) * 512)
                        for kc in range(8):
                            MM(PB[i], wsrc[0].ap[:, wsrc[1], kc, :], hT.ap[:, kc, cols], kc == 0, kc == 7,
                               [wsrc[0].t, hT.t[tt]], PT[i])

                    def p1_rest(n):
                        j, tt = tiles[n]
                        i = n % 2
                        cc = 6 * g + j
                        cols = slice(tt * 512, (tt + 1) * 512)
                        up = upad[i]
                        CP(up.ap[:, 0:3], tails.ap[:, cc, :], tails.t[cc], up.t, raw=True)
                        CP(up.ap[:, 3:515], PB[i], PT[i], up.t, eng="act")
                        CP(tails.ap[:, cc, :], up.ap[:, 512:515], up.t, tails.t[cc], raw=True)
                        for k in range(4):
                            MM(PB[2 + i], dg.ap[:, j, k, :], up.ap[:, k:k + 512], k == 0, k == 3, [dg.t, up.t], PT[2 + i])
                        if j < 4:
                            dst, dt_tok = xsT.ap[:, j, cols], xsT.t
                        elif j == 4:
                            dst, dt_tok = BT.ap[:, cols], BT.t
                        else:
                            dst, dt_tok = CT.ap[:, cols], CT.t
                        ACT(dst, PB[2 + i], AF.Silu, [PT[2 + i], vecs.t], dt_tok, bias=cb[:, cc:cc + 1])

                    p1_mm(0)
                    for n in range(12):
                        if n + 1 < 12:
                            p1_mm(n + 1)
                        p1_rest(n)
                    if hf == 0 and g == 0:
                        DUMP("xsT", xsT.ap[:, 0, 0:128], xsT.t)
                    for c in range(8):
                        i = c % 2
                        cols = slice(c * 128, (c + 1) * 128)
                        for kc in range(8):
                            MM(PB[i], hT.ap[:, kc, cols], wz.ap[:, kc, :], kc == 0, kc == 7, [hT.t[c // 4], wz.t], PT[i])
                        ACT(sz_all.ap[:, c, :], PB[i], AF.Silu, PT[i], gtok(sz_all, c * 1024, (c + 1) * 1024))
                    if g + 1 < 4:
                        wnext = s_loadw(g + 1)
                    ar_reset(off_s)
                    xsD = [ar([8, 64], BF16) for _ in range(2)]
                    xdt = [ar([8, 64], BF16) for _ in range(2)]
                    xdte = [ar([8, 64], BF16) for _ in range(2)]
                    B_tok = [ar([128], BF16) for _ in range(2)]
                    cbTm = [ar([128], BF16) for _ in range(2)]
                    MT = [ar([8, 128], BF16) for _ in range(2)]
                    rhsS1 = ar([8, 128], F32)
                    rhsS = [rhsS1, rhsS1]
                    E = ar([8, 128], BF16)
                    y = ar([8, 64], F32)
                    sq = ar([512], F32)
                    yn = [ar([512], BF16) for _ in range(2)]

                    def s_front(c):
                        k = c % 2
                        cols = slice(c * 128, (c + 1) * 128)
                        TT(rhsS[k].ap, tri_f.ap[:].unsqueeze(1).broadcast_to([128, 8, 128]),
                           dA.ap[:, c, hs].unsqueeze(2).broadcast_to([128, 8, 128]), ALU.mult, [tri_f.t, dA.t], rhsS[k].t, eng="pool")
                        for i in range(4):
                            TR(PBb[2][:, i * 128:(i + 1) * 128], xsT.ap[:, i, cols], ident_b.ap[:], [xsT.t, ident_b.t], PT[2])
                        TR(PBb[2][:, 512:640], BT.ap[:, cols], ident_b.ap[:], [BT.t, ident_b.t], PT[2])
                        TT(xsD[k].ap, v8(PBb[2][:, 0:512]), D_bc[:, hs].unsqueeze(2).broadcast_to([128, 8, 64]), ALU.mult,
                           [PT[2], rows.t], xsD[k].t)
                        TT(xdt[k].ap, v8(PBb[2][:, 0:512]), dt_.ap[:, c, hs].unsqueeze(2).broadcast_to([128, 8, 64]), ALU.mult,
                           [PT[2], dt_.t], xdt[k].t)
                        TT(xdte[k].ap, v8(PBb[2][:, 0:512]), dtdte.ap[:, c, hs].unsqueeze(2).broadcast_to([128, 8, 64]), ALU.mult,
                           [PT[2], dtdte.t], xdte[k].t)
                        CP(B_tok[k].ap, PBb[2][:, 512:640], PT[2], B_tok[k].t, raw=True)
                        MM(PB[3][:, 0:128], BT.ap[:, cols], CT.ap[:, cols], True, True, [BT.t, CT.t], PT[3])
                        TT(cbTm[k].ap, PB[3][:, 0:128], tri_f.ap[:], ALU.mult, [PT[3], tri_f.t], cbTm[k].t)
                        for q in range(2):
                            MM(PB[4 + q], ugt_f.ap[:], rhsS[k].ap[:, 4 * q:4 * q + 4, :].rearrange("p a b -> p (a b)"), True, True,
                               [ugt_f.t, rhsS[k].t], PT[4 + q])
                            ACT(E.ap[:, 4 * q:4 * q + 4, :].rearrange("p a b -> p (a b)"), PB[4 + q], AF.Exp, PT[4 + q], E.t)
                        TT(MT[k].ap, E.ap, cbTm[k].ap.unsqueeze(1).broadcast_to([128, 8, 128]), ALU.mult, [E.t, cbTm[k].t], MT[k].t)

                    def s_tail(c):
                        k = c % 2
                        cols = slice(c * 128, (c + 1) * 128)
                        MM(PB[6], ident_b.ap[:], xsD[k].ap.rearrange("p a b -> p (a b)"), True, False, [ident_b.t, xsD[k].t], PT[6])
                        for h in range(8):
                            MM(PB[6][:, h * 64:(h + 1) * 64], MT[k].ap[:, h, :], xdt[k].ap[:, h, :], False, h == 7,
                               [MT[k].t, xdt[k].t], PT[6])
                        MM(PB[7], CT.ap[:, cols], Hin.ap[:, g, :], True, True, [CT.t, Hin.t[g]], PT[7])
                        TT(y.ap, v8(PB[7]), eacs.ap[:, c, hs].unsqueeze(2).broadcast_to([128, 8, 64]), ALU.mult, [PT[7], eacs.t], y.t)
                        TT(y.ap, y.ap, v8(PB[6]), ALU.add, [y.t, PT[6]], y.t)
                        yf = y.ap.rearrange("p a b -> p (a b)")
                        ygt = gtok(yg_all, c * 1024, (c + 1) * 1024)
                        TT(yg_all.ap[:, c, :], yf, sz_all.ap[:, c, :], ALU.mult, [y.t, gtok(sz_all, c * 1024, (c + 1) * 1024)], ygt)
                        ACT(sq.ap, yg_all.ap[:, c, :], AF.Square, ygt, [sq.t, ssg_all.t], accum=ssg_all.ap[:, c:c + 1])
                        MM(PB[0], B_tok[k].ap, xdte[k].ap.rearrange("p a b -> p (a b)"), True, True, [B_tok[k].t, xdte[k].t], PT[0])
                        TT(v8(Sst.ap[:, g, :]), v8(Sst.ap[:, g, :]), cd.ap[:, c, hs].unsqueeze(2).broadcast_to([128, 8, 64]), ALU.mult,
                           [Sst.t[g], cd.t], Sst.t[g], eng="pool")
                        TT(Sst.ap[:, g, :], Sst.ap[:, g, :], PB[0], ALU.add, [Sst.t[g], PT[0]], Sst.t[g])
                        CP(Hin.ap[:, g, :], Sst.ap[:, g, :], Sst.t[g], Hin.t[g], eng="act")

                    s_front(0)
                    for c in range(8):
                        if c + 1 < 8:
                            s_front(c + 1)
                        s_tail(c)
                    ACT(rsg_all.ap, ssg_all.ap, AF.Ln, [ssg_all.t, epsb.t], rsg_all.t, bias=epsb.ap[:], scale=1.0 / 512)
                    ACT(rsg_all.ap, rsg_all.ap, AF.Exp, rsg_all.t, rsg_all.t, scale=-0.5)

                    def p3_scale(c):
                        TS(yn[c % 2].ap, yg_all.ap[:, c, :], rsg_all.ap[:, c:c + 1], None, ALU.mult, None,
                           [gtok(yg_all, c * 1024, (c + 1) * 1024), rsg_all.t], yn[c % 2].t)

                    p3_scale(0)
                    for c in range(8):
                        k = c % 2
                        bk = 1 + 2 * k
                        cols = slice(c * 128, (c + 1) * 128)
                        if c + 1 < 8:
                            p3_scale(c + 1)
                        for i in range(4):
                            TR(PBb[bk][:, i * 128:(i + 1) * 128], yn[k].ap[:, i * 128:(i + 1) * 128], ident_b.ap[:],
                               [yn[k].t, ident_b.t], PT[bk])
                        TT(big.ap[:, 4 * g:4 * g + 4, cols], PBb[bk][:, 0:512].rearrange("p (a b) -> p a b", a=4, b=128),
                           nw[:, 4 * g:4 * g + 4].unsqueeze(2).broadcast_to([128, 4, 128]), ALU.mult, [PT[bk], vecs.t],
                           [yT_t(4 * g + i, c // 4) for i in range(4)])
                if hf == 0:
                    DUMP("yT", big.ap[:, 0:16, 0:128], big.t)
                if upto == "S":
                    continue

                ar_reset()
                qT = [ar([TH], BF16) for _ in range(2)]
                if hf == 1:
                    kcur = [ar([TH], BF16) for _ in range(2)]
                    vcur = [ar([8, 130], BF16) for _ in range(2)]
                    for i in range(2):
                        MSET(vcur[i].ap, 1.0, vcur[i].t)
                qkf = ar([8, 256], F32)
                qkb = ar([8, 256], BF16)
                rt = [ar([8, 4, 8], F32) for _ in range(2)]
                Pm = [ar([512], BF16) for _ in range(4)]
                Oacc = ar([8, 2, 130], F32)
                rec = ar([8, 2], F32)
                nl2 = ar([8], F32)
                o1 = ar([8, 128], F32)
                o2 = ar([8, 128], F32)
                sso = ar([8], F32)
                rso = ar([8], F32)
                on = [ar([8, 128], BF16) for _ in range(2)]

                def gtoks(buf, lo, hi):
                    return buf.t[lo // (2 * GRAN):(hi + 2 * GRAN - 1) // (2 * GRAN)]

                def a_inproj(h):
                    hp = h % 2
                    wq = wload(wqkv_d[:, h], [8, 384])

                    def blk(b):
                        i = 2 * (b % 2)
                        bcols = slice(b * 128, (b + 1) * 128)
                        for kc in range(8):
                            MM(PB[i][:, 0:384], hT.ap[:, kc, bcols], wq.ap[:, kc, :], kc == 0, kc == 7, [hT.t[b // 4], wq.t], PT[i])
                        TS(qkf.ap[:, b, :], PB[i][:, 0:256], 1.0, None, ALU.mult, None, PT[i], gtoks(qkf, b * 1024, (b + 1) * 1024))
                        if hf == 0:
                            vdst, vtok = vcache.ap[:, h, b, 0:128], vcache.t[h * 8 + b]
                        else:
                            vdst, vtok = vcur[hp].ap[:, b, 0:128], vcur[hp].t
                        TS(vdst, PB[i][:, 256:384], 1.0, None, ALU.mult, None, PT[i], vtok)

                    return [(lambda b=b: blk(b)) for b in range(8)]

                def a_rope(h):
                    gb0 = hf * 8
                    s5 = qkf.ap.rearrange("p b (s d) -> p b s d", s=4, d=64)
                    d5 = qkb.ap.rearrange("p b (s d) -> p b s d", s=4, d=64)
                    cosb = sc.ap[:, gb0:gb0 + 8, 8:16].unsqueeze(2).broadcast_to([128, 8, 4, 8])
                    sinb = sc.ap[:, gb0:gb0 + 8, 0:8].unsqueeze(2).broadcast_to([128, 8, 4, 8])
                    x1 = s5[:, :, :, 0:8]
                    x2 = s5[:, :, :, 8:16]
                    return [
                        lambda: TS(qkb.ap.rearrange("p b c -> p (b c)"), qkf.ap.rearrange("p b c -> p (b c)"), 1.0, None, ALU.mult, None, qkf.t, qkb.t),
                        lambda: TT(rt[0].ap, x1, cosb, ALU.mult, [qkf.t, sc.t], rt[0].t),
                        lambda: TT(rt[1].ap, x2, sinb, ALU.mult, [qkf.t, sc.t], rt[1].t),
                        lambda: TT(d5[:, :, :, 0:8], rt[0].ap, rt[1].ap, ALU.subtract, [rt[0].t, rt[1].t], qkb.t),
                        lambda: TT(rt[0].ap, x2, cosb, ALU.mult, [qkf.t, sc.t], rt[0].t),
                        lambda: TT(rt[1].ap, x1, sinb, ALU.mult, [qkf.t, sc.t], rt[1].t),
                        lambda: TT(d5[:, :, :, 8:16], rt[0].ap, rt[1].ap, ALU.add, [rt[0].t, rt[1].t], qkb.t),
                    ]

                def a_transposes(h):
                    hp = h % 2
                    pv4 = PBb[2].rearrange("p (j t c) -> p j t c", j=4, t=2, c=128)
                    for h4 in range(2):
                        for j in range(4):
                            b = h4 * 4 + j
                            TR(PBb[2][:, j * 256:j * 256 + 128], qkb.ap[:, b, 0:128], ident_b.ap[:], [qkb.t, ident_b.t], PT[2])
                            TR(PBb[2][:, j * 256 + 128:j * 256 + 256], qkb.ap[:, b, 128:256], ident_b.ap[:], [qkb.t, ident_b.t], PT[2])
                        c4 = slice(h4 * 512, (h4 + 1) * 512)
                        CP(qT[hp].ap[:, c4].rearrange("p (j c) -> p j c", j=4, c=128), pv4[:, :, 0, :], PT[2], qT[hp].t, raw=True)
                        if hf == 0:
                            kd, kt = kcache.ap[:, h, c4], [kcache.t[h * 8 + h4 * 4 + j] for j in range(4)]
                        else:
                            kd, kt = kcur[hp].ap[:, c4], kcur[hp].t
                        CP(kd.rearrange("p (j c) -> p j c", j=4, c=128), pv4[:, :, 1, :], PT[2], kt, raw=True)
                    if hf == 0 and h == 0:
                        DUMP("qT", qT[0].ap[:, 0:128], qT[0].t)

                nsc = [0]

                def a_attn(h, qbs):
                    hp = h % 2

                    def ksrc(kb):
                        if kb < 8:
                            return kcache.ap[:, h, kb * 128:(kb + 1) * 128], kcache.t[h * 8 + kb]
                        return kcur[hp].ap[:, (kb - 8) * 128:(kb - 7) * 128], kcur[hp].t

                    def vsrc(kb):
                        if kb < 8:
                            return vcache.ap[:, h, kb, 0:129], vcache.t[h * 8 + kb]
                        return vcur[hp].ap[:, kb - 8, 0:129], vcur[hp].t

                    groups = []
                    for qb in qbs:
                        gq = hf * 8 + qb
                        nkb = gq + 1
                        ng = (nkb + 3) // 4
                        for kg in range(ng):
                            kbs = list(range(4 * kg, min(4 * kg + 4, nkb)))
                            groups.append((qb, gq, nkb, kbs, kg == ng - 1))
                    SB = ((3, 4), (5, 1))

                    def s1(gi):
                        qb, gq, nkb, kbs, last = groups[gi]
                        banks = SB[(nsc[0] + gi) % 2]
                        qcols = slice(qb * 128, (qb + 1) * 128)
                        for jj, kb in enumerate(kbs):
                            ka, kt = ksrc(kb)
                            for m in range(2):
                                pr = slice(64 * m, 64 * m + 64)
                                MM(PB[banks[m]][:, jj * 128:(jj + 1) * 128], ka[pr, :], qT[hp].ap[pr, qcols], True, True,
                                   [kt, qT[hp].t], PT[banks[m]])

                    def s23(gi):
                        qb, gq, nkb, kbs, last = groups[gi]
                        banks = SB[(nsc[0] + gi) % 2]
                        w_ = len(kbs) * 128
                        for m in range(2):
                            pm = Pm[2 * ((nsc[0] + gi) % 2) + m]
                            ACT(pm.ap[:, 0:w_], PB[banks[m]][:, 0:w_], AF.Exp, PT[banks[m]], pm.t, scale=0.125)
                            if gq in kbs:
                                jj = kbs.index(gq)
                                TT(pm.ap[:, jj * 128:(jj + 1) * 128], pm.ap[:, jj * 128:(jj + 1) * 128], tri_b.ap[:], ALU.mult,
                                   [pm.t, tri_b.t], pm.t)
                            for jj, kb in enumerate(kbs):
                                va, vt = vsrc(kb)
                                MM(PB[6 + m][:, 0:129], pm.ap[:, jj * 128:(jj + 1) * 128], va, kb == 0, kb == nkb - 1,
                                   [pm.t, vt], PT[6 + m])
                        if last:
                            ot_ = gtoks(Oacc, qb * 1040, (qb + 1) * 1040)
                            TS(Oacc.ap[:, qb, 0, 0:129], PB[6][:, 0:129], 1.0, None, ALU.mult, None, PT[6], ot_)
                            TS(Oacc.ap[:, qb, 1, 0:129], PB[7][:, 0:129], 1.0, None, ALU.mult, None, PT[7], ot_)

                    s1(0)
                    for gi in range(len(groups)):
                        if gi + 1 < len(groups):
                            s1(gi + 1)
                        s23(gi)
                        for _ in range(2):
                            if pend:
                                pend.pop(0)()
                    nsc[0] += len(groups)

                def a_epi_dve(h):
                    hp = h % 2
                    b8 = lambda ap: ap.unsqueeze(2).broadcast_to([128, 8, 128])
                    RCP(rec.ap, Oacc.ap[:, :, :, 128], Oacc.t, rec.t)
                    TS(nl2.ap, rec.ap[:, :, 1], neglam.ap[:], None, ALU.mult, None, [rec.t, neglam.t], nl2.t)
                    TT(o1.ap, Oacc.ap[:, :, 0, 0:128], b8(rec.ap[:, :, 0]), ALU.mult, [Oacc.t, rec.t], o1.t)
                    TT(o2.ap, Oacc.ap[:, :, 1, 0:128], b8(nl2.ap), ALU.mult, [Oacc.t, nl2.t], o2.t)
                    return [
                        lambda: TT(o1.ap, o1.ap, o2.ap, ALU.add, [o1.t, o2.t], o1.t),
                        lambda: TT(o2.ap, o1.ap, o1.ap, ALU.mult, o1.t, o2.t),
                        lambda: TRED(sso.ap, o2.ap, o2.t, sso.t),
                        lambda: ACT(rso.ap, sso.ap, AF.Ln, [sso.t, epsb.t], rso.t, bias=epsb.ap[:], scale=1.0 / 128),
                        lambda: ACT(rso.ap, rso.ap, AF.Exp, rso.t, rso.t, scale=-0.5),
                        lambda: TT(on[hp].ap, o1.ap, b8(rso.ap), ALU.mult, [o1.t, rso.t], on[hp].t),
                    ]

                def a_epi_tr(h):
                    hp = h % 2
                    for h4 in range(2):
                        for j in range(4):
                            TR(PBb[2][:, j * 128:(j + 1) * 128], on[hp].ap[:, h4 * 4 + j, :], ident_b.ap[:], [on[hp].t, ident_b.t], PT[2])
                        TS(big.ap[:, 16 + h, h4 * 512:(h4 + 1) * 512], PBb[2][:, 0:512], swl.ap[:], None, ALU.mult, None,
                           [PT[2], swl.t], yT_t(16 + h, h4))

                pend = []

                def flush():
                    while pend:
                        pend.pop(0)()

                pend.extend(a_inproj(0))
                pend.extend(a_rope(0))
                flush()
                a_transposes(0)
                for h in range(8):
                    if h + 1 < 8:
                        pend.extend(a_inproj(h + 1))
                        pend.extend(a_rope(h + 1))
                    a_attn(h, range(0, 8))
                    flush()
                    if h > 0:
                        a_epi_tr(h - 1)
                    if h + 1 < 8:
                        a_transposes(h + 1)
                    pend.extend(a_epi_dve(h))
                flush()
                a_epi_tr(7)
                if hf == 0:
                    DUMP("oT", big.ap[:, 16:24, 0:128], big.t)
                if upto == "A":
                    continue

                ar_reset()
                mixT = ar([8, TH], BF16)
                s1 = ar([512], F32)
                s2 = ar([512], F32)
                n = 0
                for c in range(8):
                    wc = wload(wc_d[:, c], [40, 128])
                    for tt in range(2):
                        cols = slice(tt * 512, (tt + 1) * 512)
                        b0 = 4 * (n % 2)
                        n += 1
                        for kc in range(16):
                            MM(PB[b0], wc.ap[:, kc, :], big.ap[:, kc, cols], kc == 0, kc == 15, [wc.t, yT_t(kc, tt)], PT[b0])
                        for kc in range(8):
                            MM(PB[b0 + 1], wc.ap[:, 16 + kc, :], big.ap[:, 16 + kc, cols], kc == 0, kc == 7, [wc.t, yT_t(16 + kc, tt)],
                               PT[b0 + 1])
                        for kc in range(8):
                            MM(PB[b0 + 2], wc.ap[:, 24 + kc, :], hT.ap[:, kc, cols], kc == 0, kc == 7, [wc.t, hT.t[tt]], PT[b0 + 2])
                        for kc in range(8):
                            MM(PB[b0 + 3], wc.ap[:, 32 + kc, :], hT.ap[:, kc, cols], kc == 0, kc == 7, [wc.t, hT.t[tt]], PT[b0 + 3])
                        ACT(s1.ap, PB[b0 + 2], AF.Sigmoid, PT[b0 + 2], s1.t)
                        ACT(s2.ap, PB[b0 + 3], AF.Sigmoid, PT[b0 + 3], s2.t)
                        TT(s1.ap, s1.ap, PB[b0], ALU.mult, [s1.t, PT[b0]], s1.t)
                        TT(s2.ap, s2.ap, PB[b0 + 1], ALU.mult, [s2.t, PT[b0 + 1]], s2.t)
                        TT(mixT.ap[:, c, cols], s1.ap, s2.ap, ALU.add, [s1.t, s2.t], mixT.t)
                if hf == 0:
                    DUMP("mixT", mixT.ap[:, :, 0:128], mixT.t)
                if upto == "C":
                    continue

                g2bc = ar([D], F32)
                P.dma("sp", g2bc.ap, g24_d[0].partition_broadcast(128), (), g2bc.t)
                xb = [ar([D], F32) for _ in range(2)]
                x2t = [ar([D], F32) for _ in range(2)]
                xs = [ar([D], BF16) for _ in range(2)]
                junk = ar([D], F32)
                ss = [ar([1], F32) for _ in range(2)]
                rstd = [ar([1], F32) for _ in range(2)]
                wm = [wload(wmix_d[:, nh], [8, 512]) for nh in range(2)]
                out_tok = [P.tok(f"out{b}") for b in range(8)]
                def m_mm(b):
                    bk = 2 * (b % 2)
                    bcols = slice(b * 128, (b + 1) * 128)
                    for nh in range(2):
                        for kc in range(8):
                            MM(PB[bk + nh], mixT.ap[:, kc, bcols], wm[nh].ap[:, kc, :], kc == 0, kc == 7, [mixT.t, wm[nh].t], PT[bk + nh])

                def m_chain(b):
                    i = b % 2
                    bk = 2 * i
                    rowsl = slice(t0 + b * 128, t0 + (b + 1) * 128)
                    pm2 = ps[:, bk:bk + 2, :].rearrange("p a b -> p (a b)")
                    ptk = [PT[bk], PT[bk + 1]]
                    P.dma("sp", xb[i].ap, x_d[rowsl, :], (), xb[i].t)
                    rms_rstd(pm2, D, ptk, junk, ss[i], rstd[i])
                    STT(x2t[i].ap, pm2, rstd[i].ap, g2bc.ap, ALU.mult, ALU.mult, [ptk, rstd[i].t, g2bc.t], x2t[i].t)
                    TT(x2t[i].ap, x2t[i].ap, xb[i].ap, ALU.add, [x2t[i].t, xb[i].t], x2t[i].t)
                    P.dma("act", out_d[rowsl, :], x2t[i].ap, x2t[i].t, out_tok[b])
                    rms_rstd(x2t[i].ap, D, x2t[i].t, junk, ss[i], rstd[i])
                    TS(xs[i].ap, x2t[i].ap, rstd[i].ap, None, ALU.mult, None, [x2t[i].t, rstd[i].t], xs[i].t)

                m_mm(0)
                for b in range(8):
                    if b + 1 < 8:
                        m_mm(b + 1)
                    m_chain(b)
                    to_hT(xs[b % 2], b, g3, 4 + b % 2)
                if upto == "M":
                    continue

                ar_reset()
                fT = ar([8, 512], F32)
                g4bc = ar([D], F32)
                P.dma("sp", g4bc.ap, g24_d[1].partition_broadcast(128), (), g4bc.t)
                sg = [ar([512], F32) for _ in range(2)]
                x2b = [ar([D], F32) for _ in range(2)]
                ot = [ar([D], F32) for _ in range(2)]
                junk = ar([D], F32)
                ss = [ar([1], F32) for _ in range(2)]
                rstd = [ar([1], F32) for _ in range(2)]
                n = 0
                for hg in range(NHC // 2):
                    wg = wload(wgu_d[:, 2 * hg:2 * hg + 2], [2, 16, 128])
                    for j in range(2):
                        hc = 2 * hg + j
                        for tt in range(2):
                            cols = slice(tt * 512, (tt + 1) * 512)
                            i = n % 2
                            bk = 2 * i
                            n += 1
                            for kc in range(8):
                                MM(PB[bk], wg.ap[:, j, kc, :], hT.ap[:, kc, cols], kc == 0, kc == 7, [wg.t, hT.t[tt]], PT[bk])
                            for kc in range(8):
                                MM(PB[bk + 1], wg.ap[:, j, 8 + kc, :], hT.ap[:, kc, cols], kc == 0, kc == 7, [wg.t, hT.t[tt]], PT[bk + 1])
                            ACT(sg[i].ap, PB[bk], AF.Silu, PT[bk], sg[i].t)
                            TT(big.ap[:, hc, cols], sg[i].ap, PB[bk + 1], ALU.mult, [sg[i].t, PT[bk + 1]], yT_t(hc, tt))
                for tt in range(2):
                    cols = slice(tt * 512, (tt + 1) * 512)
                    for c in range(8):
                        wdn = wload(wd_d[:, c], [NHC, 128])
                        bk = 4 + c % 2
                        for hc in range(NHC):
                            MM(PB[bk], wdn.ap[:, hc, :], big.ap[:, hc, cols], hc == 0, hc == NHC - 1, [wdn.t, yT_t(hc, tt)], PT[bk])
                        CP(fT.ap[:, c, :], PB[bk], PT[bk], fT.t, eng="act")
                    def f_tr(b4):
                        pb0 = 6 if b4 % 2 == 0 else 0
                        for c in range(8):
                            TR(PB[pb0 + c // 4][:, (c % 4) * 128:(c % 4 + 1) * 128], fT.ap[:, c, b4 * 128:(b4 + 1) * 128], ident_f.ap[:],
                               [fT.t, ident_f.t], PT[pb0 + c // 4])

                    def f_chain(b4):
                        b = tt * 4 + b4
                        i = b % 2
                        pb0 = 6 if b4 % 2 == 0 else 0
                        rowsl = slice(t0 + b * 128, t0 + (b + 1) * 128)
                        pf = ps[:, pb0:pb0 + 2, :].rearrange("p a b -> p (a b)")
                        ptk = [PT[pb0], PT[pb0 + 1]]
                        P.dma("sp", x2b[i].ap, out_d[rowsl, :], out_tok[b], x2b[i].t)
                        rms_rstd(pf, D, ptk, junk, ss[i], rstd[i])
                        STT(ot[i].ap, pf, rstd[i].ap, g4bc.ap, ALU.mult, ALU.mult, [ptk, rstd[i].t, g4bc.t], ot[i].t)
                        TT(ot[i].ap, ot[i].ap, x2b[i].ap, ALU.add, [ot[i].t, x2b[i].t], ot[i].t)
                        P.dma("act", out_d[rowsl, :], ot[i].ap, ot[i].t, out_tok[b])

                    f_tr(0)
                    for b4 in range(4):
                        if b4 + 1 < 4:
                            f_tr(b4 + 1)
                        f_chain(b4)

        except _Stop:
            pass
        P.emit()
        build.stats = (P.stats, P.nwaits)
    return nc


def _tile_w(W, ncol):
    K, N = W.shape
    return np.ascontiguousarray(W.reshape(K // 128, 128, N // ncol, ncol).transpose(1, 2, 0, 3))


def _prep_shared(inp):
    w_in = np.asarray(inp["w_in"][0], dtype=np.float32)
    sh = {}
    xbc_t = _tile_w(w_in[:, X0:DT0], 128)
    order = []
    for g in range(4):
        order += [4 * g, 4 * g + 1, 4 * g + 2, 4 * g + 3, 16 + g, 20 + g]
    sh["w_xbc"] = np.ascontiguousarray(xbc_t[:, order])
    sh["w_z"] = _tile_w(w_in[:, Z0:X0], 512)
    sh["w_dt"] = np.ascontiguousarray(_tile_w(w_in[:, DT0:Q0], 32)[:, 0])
    qkv = np.concatenate([w_in[:, Q0:K0].reshape(D, 8, 128), w_in[:, K0:V0].reshape(D, 8, 128),
                          w_in[:, V0:G0].reshape(D, 8, 128)], axis=2).reshape(D, 8 * 384)
    sh["w_qkv"] = _tile_w(qkv, 384)
    wso = _tile_w(np.asarray(inp["w_ssm_out"][0], np.float32), 128)
    wao = _tile_w(np.asarray(inp["w_attn_out"][0], np.float32), 128)
    wgs = _tile_w(w_in[:, G0:G0 + D], 128)
    wga = _tile_w(w_in[:, G0 + D:G0 + 2 * D], 128)
    sh["w_c"] = np.ascontiguousarray(np.concatenate([wso, wao, wgs, wga], axis=2))
    sh["w_mix"] = _tile_w(np.asarray(inp["w_mix_out"][0], np.float32), 512)
    wg = _tile_w(np.asarray(inp["w_ffn_gate"][0], np.float32), 128)
    wu = _tile_w(np.asarray(inp["w_ffn_up"][0], np.float32), 128)
    sh["w_gu"] = np.ascontiguousarray(np.concatenate([wg, wu], axis=2))
    sh["w_d"] = _tile_w(np.asarray(inp["w_ffn_down"][0], np.float32), 128)

    def pc(v, n):
        return np.asarray(v, np.float32).reshape(n, 128).T

    conv_w = np.asarray(inp["conv_w"][0], np.float32)
    cwt = conv_w.T.reshape(24, 128, 4).transpose(1, 0, 2)[:, order]
    cbt = pc(inp["conv_b"][0], 24)[:, order]
    sh["vecs"] = np.ascontiguousarray(np.concatenate([
        pc(inp["norm_pre_mix"][0], 8), pc(inp["norm_pre_ffn"][0], 8), pc(inp["ssm_norm_w"][0], 16), cbt,
        cwt.reshape(128, 96), np.asarray(inp["attn_subln_w"][0], np.float32).reshape(128, 1)], axis=1))
    sh["rows"] = np.ascontiguousarray(np.concatenate([
        np.asarray(inp[k][0], np.float32) for k in ("dt_bias", "a_log", "d_skip", "lam_q1", "lam_k1", "lam_q2", "lam_k2")]))
    sh["g24"] = np.ascontiguousarray(np.stack([np.asarray(inp["norm_post_mix"][0], np.float32),
                                               np.asarray(inp["norm_post_ffn"][0], np.float32)]))
    return sh


def _in_maps(inp, cores):
    sh = _prep_shared(inp)
    maps = []
    for c in cores:
        m = dict(sh)
        m["x"] = np.ascontiguousarray(np.asarray(inp["x"][c], np.float32))
        m["posr"] = np.ascontiguousarray(np.asarray(inp["positions"][c], np.int32).reshape(16, 128).T)
        maps.append(m)
    return maps


_NC = None


def kernel(**inputs):
    global _NC
    if _NC is None:
        _NC = build()
    maps = _in_maps(inputs, list(range(8)))
    res = run_bass_kernel_spmd(_NC, maps, core_ids=list(range(8)))
    return np.stack([np.asarray(r["out"], np.float32) for r in res.results], axis=0)
```

```python
import math
import numpy as np
from contextlib import ExitStack
import concourse.bass as bass
import concourse.mybir as mybir
from concourse.bass_utils import run_bass_kernel_spmd

F32 = mybir.dt.float32
BF16 = mybir.dt.bfloat16
I32 = mybir.dt.int32
ALU = mybir.AluOpType
AF = mybir.ActivationFunctionType
AX = mybir.AxisListType

D = 1024
S = 2048
TH = 1024
NHALF = 2
FH = 2816
NHC = 22
Z0, X0, DT0, Q0, K0, V0, G0 = 0, 2048, 5120, 5152, 6176, 7200, 8224
EPS = 1e-6
LAM_INIT = 0.8 - 0.6 * math.exp(-0.0)
WSLOT = 5120
NWS = 3
ARENA = 28 * 1024
GRAN = 256


class Tok:
    __slots__ = ("name", "w", "r", "excl")

    def __init__(self, name="", excl=False):
        self.name = name
        self.w = None
        self.r = []
        self.excl = excl


class Op:
    __slots__ = ("eng", "fn", "deps", "sig", "tick", "dma", "dsem", "dcnt", "idx", "know", "waits")


def _flat(x):
    out = []
    if isinstance(x, Tok):
        return [x]
    for t in x:
        if isinstance(t, (list, tuple)):
            out.extend(_flat(t))
        elif t is not None:
            out.append(t)
    return out


class Prog:
    ENGS = ("pe", "act", "dve", "pool", "sp")
    ND = 8

    def __init__(self, nc, es):
        self.nc = nc
        self.es = es
        self.by_eng = {e: [] for e in self.ENGS}
        self.all = []

    def tok(self, name=""):
        return Tok(name)

    def add(self, eng, fn, reads=(), writes=(), dma=False):
        reads = _flat(reads)
        writes = _flat(writes)
        op = Op()
        op.eng = eng
        op.fn = fn
        op.dma = dma
        op.sig = False
        op.tick = 0
        deps = set()
        for t in reads:
            if t.w is not None:
                deps.add(t.w)
            if t.excl:
                deps.update(r_ for r_ in t.r if r_.eng != eng)
        for t in writes:
            if t.w is not None:
                deps.add(t.w)
            deps.update(t.r)
        for t in reads:
            t.r.append(op)
        for t in writes:
            t.w = op
            t.r = []
        deps.discard(op)
        if eng == "pe" and not dma:
            deps = {d for d in deps if d.dma or d.eng != "pe"}
        for d in deps:
            if not d.dma:
                d.sig = True
        op.deps = deps
        op.idx = len(self.all)
        self.all.append(op)
        self.by_eng[eng].append(op)
        return op

    def dma(self, eng, out, in_, reads=(), writes=()):
        return self.add(eng, lambda e: e.dma_start(out=out, in_=in_), reads, writes, dma=True)

    def emit(self):
        nc, es = self.nc, self.es
        sem = {e: es.enter_context(nc.semaphore(f"s_{e}")) for e in self.ENGS}
        rings = {}
        for e in self.ENGS:
            cnt = 0
            k = 0
            ring = None
            counts = None
            for op in self.by_eng[e]:
                if op.dma:
                    if ring is None:
                        ring = [es.enter_context(nc.semaphore(f"d_{e}{i}")) for i in range(self.ND)]
                        counts = [0] * self.ND
                        rings[e] = (ring, counts)
                    i = k % self.ND
                    k += 1
                    counts[i] += 16
                    op.dsem = ring[i]
                    op.dcnt = counts[i]
                elif op.sig:
                    cnt += 1
                    op.tick = cnt
        self.stats = {e: len(v) for e, v in self.by_eng.items()}
        nwaits = {e: 0 for e in self.ENGS}
        state = {e: {} for e in self.ENGS}
        for op in self.all:
            st = state[op.eng]
            waits = []

            def need(s_, v, src):
                if st.get(id(s_), 0) >= v:
                    return
                waits.append((s_, v))
                st[id(s_)] = v
                if src is not None:
                    for k_, val in src.items():
                        if st.get(k_, 0) < val:
                            st[k_] = val

            for d in sorted(op.deps, key=lambda d_: -d_.idx):
                if d.dma:
                    need(d.dsem, d.dcnt, d.know)
                else:
                    need(sem[d.eng], d.tick, d.know)
            if op.dma and op.dcnt > 16:
                need(op.dsem, op.dcnt - 16, None)
            op.waits = waits
            op.know = dict(st)

        def run(e, eng):
            for op in self.by_eng[e]:
                for s_, v in op.waits:
                    eng.wait_ge(s_, v)
                    nwaits[e] += 1
                if op.dma:
                    op.fn(eng).then_inc(op.dsem, 16)
                else:
                    ins = op.fn(eng)
                    if op.sig:
                        ins.then_inc(sem[e], 1)
            if e in rings:
                ring, counts = rings[e]
                for s_, c in zip(ring, counts):
                    if c and state[e].get(id(s_), 0) < c:
                        eng.wait_ge(s_, c)

        with nc.Block() as block:
            @block.tensor
            def _(eng):
                run("pe", eng)

            @block.scalar
            def _(eng):
                run("act", eng)

            @block.vector
            def _(eng):
                run("dve", eng)

            @block.gpsimd
            def _(eng):
                run("pool", eng)

            @block.sync
            def _(eng):
                run("sp", eng)
        self.nwaits = nwaits


class _Stop(Exception):
    pass


class Buf:
    __slots__ = ("ap", "t")

    def __init__(self, ap, t):
        self.ap = ap
        self.t = t


def build(upto="F", dbg=()):
    nc = bass.Bass("TRN2", target_bir_lowering=False)

    def din(name, shape, dt=F32):
        return nc.dram_tensor(name, list(shape), dt, kind="ExternalInput").ap()

    x_d = din("x", [S, D])
    pos_d = din("posr", [128, 16], I32)
    wxbc_d = din("w_xbc", [128, 24, 8, 128])
    wz_d = din("w_z", [128, 4, 8, 512])
    wdt_d = din("w_dt", [128, 8, 32])
    wqkv_d = din("w_qkv", [128, 8, 8, 384])
    wc_d = din("w_c", [128, 8, 40, 128])
    wmix_d = din("w_mix", [128, 2, 8, 512])
    wgu_d = din("w_gu", [128, NHC, 16, 128])
    wd_d = din("w_d", [128, 8, NHC, 128])
    NV = 8 + 8 + 16 + 24 + 96 + 1
    vecs_d = din("vecs", [128, NV])
    NR = 32 * 3 + 64 * 4
    rows_d = din("rows", [NR])
    g24_d = din("g24", [2, D])
    out_d = nc.dram_tensor("out", [S, D], F32, kind="ExternalOutput").ap()
    dbg_d = {}
    for name, shape in dbg:
        dbg_d[name] = nc.dram_tensor("dbg_" + name, list(shape), F32, kind="ExternalOutput").ap()

    es = ExitStack()
    with es:
        P = Prog(nc, es)

        def sbt(name, shape, dt):
            return es.enter_context(nc.sbuf_tensor("sb_" + name, list(shape), dt))

        def pbuf(name, shape, dt, ntok=1):
            t = sbt(name, shape, dt)
            return Buf(t, [P.tok(name + str(i)) for i in range(ntok)])

        def MM(out, lhsT, rhs, start, stop, r, w):
            P.add("pe", lambda e: e.matmul(out, lhsT, rhs, start=start, stop=stop), r, w)

        def TR(out, in_, ident, r, w):
            P.add("pe", lambda e: e.transpose(out=out, in_=in_, identity=ident), r, w)

        def ACT(out, in_, func, r, w, bias=None, scale=None, accum=None):
            kw = {}
            if bias is not None:
                kw["bias"] = bias
            if scale is not None:
                kw["scale"] = scale
            if accum is not None:
                kw["accum_out"] = accum
            P.add("act", lambda e: e.activation(out=out, in_=in_, func=func, **kw), r, w)

        def TS(out, in0, s1, s2, op0, op1, r, w, eng="dve"):
            if op1 is None:
                P.add(eng, lambda e: e.tensor_scalar(out=out, in0=in0, scalar1=s1, scalar2=None, op0=op0), r, w)
            else:
                P.add(eng, lambda e: e.tensor_scalar(out=out, in0=in0, scalar1=s1, scalar2=s2, op0=op0, op1=op1), r, w)

        def TT(out, in0, in1, op, r, w, eng="dve"):
            P.add(eng, lambda e: e.tensor_tensor(out=out, in0=in0, in1=in1, op=op), r, w)

        def STT(out, in0, scalar, in1, op0, op1, r, w):
            P.add("dve", lambda e: e.scalar_tensor_tensor(out=out, in0=in0, scalar=scalar, in1=in1, op0=op0, op1=op1), r, w)

        def CP(out, in_, r, w, eng="dve", raw=False):
            if eng == "act":
                P.add("act", lambda e: e.copy(out=out, in_=in_), r, w)
            elif raw:
                P.add(eng, lambda e: e.tensor_copy(out=out, in_=in_), r, w)
            else:
                P.add(eng, lambda e: e.tensor_scalar(out=out, in0=in_, scalar1=1.0, scalar2=None, op0=ALU.mult), r, w)

        def gtok(buf, lo, hi):
            return buf.t[lo // (2 * GRAN):(hi + 2 * GRAN - 1) // (2 * GRAN)]

        def _stt_acc(out, in_, acc):
            return lambda e: e.scalar_tensor_tensor(out=out, in0=in_, scalar=1.0, in1=in_, op0=ALU.mult, op1=ALU.mult, accum_out=acc)

        def TRED(out, in_, r, w):
            P.add("dve", lambda e: e.tensor_reduce(out=out, in_=in_, axis=AX.X, op=ALU.add), r, w)

        def RCP(out, in_, r, w):
            P.add("dve", lambda e: e.reciprocal(out=out, in_=in_), r, w)

        def MSET(ap, val, w, eng="dve"):
            P.add(eng, lambda e: e.memset(ap, val), (), w)

        dbg_tok = P.tok("dbg")

        def DUMP(name, ap, r, rows=None):
            if name in dbg_d:
                dst = dbg_d[name] if rows is None else dbg_d[name][rows]
                P.dma("pool", dst, ap, r, [dbg_tok])

        ps = es.enter_context(nc.psum_tensor("ps", [128, 8, 512], F32))
        PB = [ps[:, i, :] for i in range(8)]
        PBb = [ps[:, i, :].bitcast(BF16) for i in range(8)]
        PT = [Tok(f"bank{i}", excl=True) for i in range(8)]

        arena_t = sbt("arena", [128, ARENA], BF16)
        arena_tok = [P.tok(f"ar{i}") for i in range(ARENA // GRAN)]
        ar_off = [0]

        def ar_reset(off=0):
            ar_off[0] = off

        def ar(shape, dt):
            n = 1
            for s_ in shape:
                n *= s_
            nb = n * (2 if dt == F32 or dt == I32 else 1)
            off = ar_off[0]
            nb_al = ((nb + GRAN - 1) // GRAN) * GRAN
            assert off + nb_al <= ARENA, ("arena overflow", off, nb_al)
            ar_off[0] = off + nb_al
            ap = arena_t[:, off:off + nb]
            if dt != BF16:
                ap = ap.bitcast(dt)
            if len(shape) == 2:
                ap = ap.rearrange("p (a b) -> p a b", a=shape[0], b=shape[1])
            elif len(shape) == 3:
                ap = ap.rearrange("p (a b c) -> p a b c", a=shape[0], b=shape[1], c=shape[2])
            return Buf(ap, arena_tok[off // GRAN:(off + nb_al) // GRAN])

        wts_t = sbt("wts", [128, NWS, WSLOT], BF16)
        wts_tok = [P.tok(f"ws{i}") for i in range(NWS)]
        wctr = [0]

        def wload(src, shape):
            i = wctr[0] % NWS
            wctr[0] += 1
            n = 1
            for s_ in shape:
                n *= s_
            assert n <= WSLOT
            ap = wts_t[:, i, 0:n]
            if len(shape) == 2:
                ap = ap.rearrange("p (a b) -> p a b", a=shape[0], b=shape[1])
            elif len(shape) == 3:
                ap = ap.rearrange("p (a b c) -> p a b c", a=shape[0], b=shape[1], c=shape[2])
            P.dma("pool", ap, src, (), [wts_tok[i]])
            return Buf(ap, [wts_tok[i]])

        ident_b = pbuf("ident_b", [128, 128], BF16)
        ident_f = pbuf("ident_f", [128, 128], F32)
        tri_f = pbuf("tri_f", [128, 128], F32)
        tri_b = pbuf("tri_b", [128, 128], BF16)
        ugt_f = pbuf("ugt_f", [128, 128], F32)
        ones_f = pbuf("ones_f", [128, 128], F32)
        vecs = pbuf("vecs", [128, NV], F32)
        rows = pbuf("rows", [128, NR], F32)
        epsb = pbuf("epsb", [128, 1], F32)
        oneb = pbuf("oneb", [128, 1], F32)
        wdt = pbuf("wdt", [128, 8, 32], BF16)
        A_bc = pbuf("A_bc", [128, 32], F32)
        neglam = pbuf("neglam", [128, 1], F32)
        swl = pbuf("swl", [128, 1], F32)
        sc = pbuf("sc", [128, 16, 16], F32)
        Sst = pbuf("Sst", [128, 4, 512], F32, 4)
        Hin = pbuf("Hin", [128, 4, 512], BF16, 4)
        tails = pbuf("tails", [128, 24, 3], BF16, 24)
        dtb = {k: pbuf("dt_" + k, [128, 8, 32], F32) for k in ("dt", "dA", "eacs", "dte", "cd", "dtdte")}
        kcache = pbuf("kcache", [128, 8, TH], BF16, 64)
        vcache = pbuf("vcache", [128, 8, 8, 130], BF16, 64)
        hT = pbuf("hT", [128, 8, TH], BF16, 2)
        big = pbuf("big", [128, 24, TH], BF16, 48)

        g1 = vecs.ap[:, 0:8]
        g3 = vecs.ap[:, 8:16]
        nw = vecs.ap[:, 16:32]
        cb = vecs.ap[:, 32:56]
        cw = vecs.ap[:, 56:152].rearrange("p (c k) -> p c k", c=24, k=4)
        sublnw = vecs.ap[:, 152:153]
        dtb_bc = rows.ap[:, 0:32]
        alog_bc = rows.ap[:, 32:64]
        D_bc = rows.ap[:, 64:96]
        lamv = rows.ap[:, 96:352]

        def yT_t(kc, tt):
            return big.t[kc * 2 + tt]

        P.dma("sp", vecs.ap[:], vecs_d, (), vecs.t)
        P.dma("sp", rows.ap[:], rows_d.partition_broadcast(128), (), rows.t)
        P.dma("pool", wdt.ap[:], wdt_d, (), wdt.t)
        MSET(epsb.ap[:], EPS, epsb.t)
        MSET(oneb.ap[:], 1.0, oneb.t)
        MSET(ones_f.ap[:], 1.0, ones_f.t)
        MSET(tails.ap[:], 0.0, tails.t)
        MSET(Sst.ap[:], 0.0, Sst.t)
        MSET(Hin.ap[:], 0.0, Hin.t)
        MSET(vcache.ap[:], 1.0, vcache.t)

        def mask(buf, pattern, base, cm, op):
            MSET(buf.ap[:], 1.0, buf.t, eng="pool")
            P.add("pool", lambda e: e.affine_select(out=buf.ap[:], in_=buf.ap[:], pattern=pattern, compare_op=op,
                                                    fill=0.0, base=base, channel_multiplier=cm), buf.t, buf.t)

        mask(ident_b, [[-1, 128]], 0, 1, ALU.is_equal)
        mask(ident_f, [[-1, 128]], 0, 1, ALU.is_equal)
        mask(tri_f, [[1, 128]], 0, -1, ALU.is_ge)
        mask(tri_b, [[1, 128]], 0, -1, ALU.is_ge)
        mask(ugt_f, [[-1, 128]], -1, 1, ALU.is_ge)

        ar_reset()
        ACT(A_bc.ap[:], alog_bc, AF.Exp, rows.t, A_bc.t)
        TS(A_bc.ap[:], A_bc.ap[:], -1.0, None, ALU.mult, None, A_bc.t, A_bc.t)
        lt = ar([256], F32)
        l2 = ar([2], F32)
        TT(lt.ap[:, 0:64], lamv[:, 0:64], lamv[:, 64:128], ALU.mult, rows.t, lt.t)
        TT(lt.ap[:, 64:128], lamv[:, 128:192], lamv[:, 192:256], ALU.mult, rows.t, lt.t)
        P.add("dve", lambda e: e.tensor_reduce(out=l2.ap[:, 0:2], in_=lt.ap[:, 0:128].rearrange("p (a b) -> p a b", a=2, b=64),
                                               axis=AX.X, op=ALU.add), lt.t, l2.t)
        ACT(l2.ap[:, 0:2], l2.ap[:, 0:2], AF.Exp, l2.t, l2.t)
        TT(neglam.ap[:], l2.ap[:, 1:2], l2.ap[:, 0:1], ALU.subtract, l2.t, neglam.t)
        TS(neglam.ap[:], neglam.ap[:], -LAM_INIT, None, ALU.add, None, neglam.t, neglam.t)
        TS(swl.ap[:], sublnw, 1.0 - LAM_INIT, None, ALU.mult, None, vecs.t, swl.t)
        posi = ar([16], I32)
        posf = ar([16], F32)
        invf = ar([16], F32)
        ang = ar([16, 16], F32)
        uu = ar([16, 16], F32)
        ki = ar([16, 16], I32)
        kf = ar([16, 16], F32)
        mk = ar([16, 16], F32)
        P.dma("sp", posi.ap[:], pos_d, (), posi.t)
        CP(posf.ap[:], posi.ap[:], posi.t, posf.t, raw=True)
        for i in range(8):
            f = float(500000.0 ** (-(2.0 * i) / 16.0))
            MSET(invf.ap[:, i:i + 1], f, invf.t)
            MSET(invf.ap[:, 8 + i:9 + i], f, invf.t)
        TT(ang.ap[:], posf.ap[:].unsqueeze(2).broadcast_to([128, 16, 16]),
           invf.ap[:].unsqueeze(1).broadcast_to([128, 16, 16]), ALU.mult, [posf.t, invf.t], ang.t)
        TS(ang.ap[:, :, 8:16], ang.ap[:, :, 8:16], math.pi / 2, None, ALU.add, None, ang.t, ang.t)
        TS(uu.ap[:], ang.ap[:], 1.0 / (2 * math.pi), None, ALU.mult, None, ang.t, uu.t)
        CP(ki.ap[:], uu.ap[:], uu.t, ki.t, raw=True)
        CP(kf.ap[:], ki.ap[:], ki.t, kf.t, raw=True)
        C1 = 6.28125
        C2 = 2 * math.pi - C1
        STT(ang.ap[:], kf.ap[:], -C1, ang.ap[:], ALU.mult, ALU.add, [kf.t, ang.t], ang.t)
        STT(ang.ap[:], kf.ap[:], -C2, ang.ap[:], ALU.mult, ALU.add, [kf.t, ang.t], ang.t)
        TS(mk.ap[:], ang.ap[:], math.pi, None, ALU.is_gt, None, ang.t, mk.t)
        STT(ang.ap[:], mk.ap[:], -2 * math.pi, ang.ap[:], ALU.mult, ALU.add, [mk.t, ang.t], ang.t)
        TS(mk.ap[:], ang.ap[:], -math.pi, None, ALU.is_lt, None, ang.t, mk.t)
        STT(ang.ap[:], mk.ap[:], 2 * math.pi, ang.ap[:], ALU.mult, ALU.add, [mk.t, ang.t], ang.t)
        TS(ang.ap[:], ang.ap[:], math.pi, -math.pi, ALU.min, ALU.max, ang.t, ang.t)
        ACT(sc.ap[:], ang.ap[:], AF.Sin, ang.t, sc.t)

        def rms_rstd(src_ap, n, r, junk, ss, rstd):
            ACT(junk.ap, src_ap, AF.Square, r, [junk.t, ss.t], accum=ss.ap)
            ACT(rstd.ap, ss.ap, AF.Ln, [ss.t, epsb.t], rstd.t, bias=epsb.ap[:], scale=1.0 / n)
            ACT(rstd.ap, rstd.ap, AF.Exp, rstd.t, rstd.t, scale=-0.5)

        def to_hT(xs, b, gvec, bank):
            pv = PBb[bank].rearrange("p (a b) -> p a b", a=8, b=128)
            for kc in range(8):
                TR(pv[:, kc, :], xs.ap[:, kc * 128:(kc + 1) * 128], ident_b.ap[:], [xs.t, ident_b.t], PT[bank])
            TT(hT.ap[:, :, b * 128:(b + 1) * 128], pv, gvec.unsqueeze(2).broadcast_to([128, 8, 128]), ALU.mult,
               [PT[bank], vecs.t], hT.t[b // 4])

        try:
            for hf in range(NHALF):
                t0 = hf * TH
                ar_reset()
                xb = [ar([D], F32) for _ in range(2)]
                xs = [ar([D], BF16) for _ in range(2)]
                junk = ar([D], F32)
                ss = [ar([1], F32) for _ in range(2)]
                rstd = [ar([1], F32) for _ in range(2)]
                def z_chain(b):
                    i = b % 2
                    P.dma("sp", xb[i].ap, x_d[t0 + b * 128:t0 + (b + 1) * 128, :], (), xb[i].t)
                    rms_rstd(xb[i].ap, D, xb[i].t, junk, ss[i], rstd[i])
                    TS(xs[i].ap, xb[i].ap, rstd[i].ap, None, ALU.mult, None, [xb[i].t, rstd[i].t], xs[i].t)

                z_chain(0)
                for b in range(8):
                    if b + 1 < 8:
                        z_chain(b + 1)
                    to_hT(xs[b % 2], b, g1, b % 2)
                if hf == 0:
                    DUMP("hT", hT.ap[:, :, 0:128], hT.t)
                if upto == "0":
                    continue

                ar_reset()
                for b in range(8):
                    for kc in range(8):
                        MM(PB[2][:, b * 32:(b + 1) * 32], hT.ap[:, kc, b * 128:(b + 1) * 128], wdt.ap[:, kc, :], kc == 0, kc == 7,
                           [hT.t[b // 4], wdt.t], PT[2])
                v3 = lambda ap: ap.rearrange("p (a b) -> p a b", a=8, b=32)
                xd = ar([8, 32], F32)
                ta = ar([8, 32], F32)
                dt_, dA, eacs, dte, cd, dtdte = (dtb[k] for k in ("dt", "dA", "eacs", "dte", "cd", "dtdte"))
                TT(xd.ap, v3(PB[2][:, 0:256]), dtb_bc.unsqueeze(1).broadcast_to([128, 8, 32]), ALU.add, [PT[2], rows.t], xd.t)
                TS(ta.ap, xd.ap, -1.0, None, ALU.mult, None, xd.t, ta.t)
                TT(ta.ap, ta.ap, xd.ap, ALU.min, [ta.t, xd.t], ta.t)
                ACT(ta.ap, ta.ap, AF.Exp, ta.t, ta.t)
                ACT(ta.ap, ta.ap, AF.Ln, [ta.t, oneb.t], ta.t, bias=oneb.ap[:])
                TS(xd.ap, xd.ap, 0.0, None, ALU.max, None, xd.t, xd.t)
                TT(dt_.ap[:], xd.ap, ta.ap, ALU.add, [xd.t, ta.t], dt_.t)
                TT(dA.ap[:], dt_.ap[:], A_bc.ap[:].unsqueeze(1).broadcast_to([128, 8, 32]), ALU.mult, [dt_.t, A_bc.t], dA.t)
                for c in range(8):
                    for bank, m in ((3, tri_f), (4, ugt_f), (5, ones_f)):
                        MM(PB[bank][:, c * 32:(c + 1) * 32], m.ap[:], dA.ap[:, c, :], True, True, [m.t, dA.t], PT[bank])
                ACT(eacs.ap[:], v3(PB[3][:, 0:256]), AF.Exp, PT[3], eacs.t)
                ACT(dte.ap[:], v3(PB[4][:, 0:256]), AF.Exp, PT[4], dte.t)
                ACT(cd.ap[:], v3(PB[5][:, 0:256]), AF.Exp, PT[5], cd.t)
                TT(dtdte.ap[:], dt_.ap[:], dte.ap[:], ALU.mult, [dt_.t, dte.t], dtdte.t)
                if hf == 0:
                    DUMP("dt", dt_.ap[:, 0, :], dt_.t)
                if upto == "S1":
                    raise _Stop

                xsT = ar([4, TH], BF16)
                BT = ar([TH], BF16)
                CT = ar([TH], BF16)
                sz_all = ar([8, 512], BF16)
                yg_all = ar([8, 512], BF16)
                ssg_all = ar([8], F32)
                rsg_all = ar([8], F32)
                off_s = ar_off[0]
                v8 = lambda ap: ap.rearrange("p (a b) -> p a b", a=8, b=64)

                def s_loadw(g):
                    return (wload(wxbc_d[:, 6 * g:6 * g + 4], [4, 8, 128]),
                            wload(wxbc_d[:, 6 * g + 4:6 * g + 6], [2, 8, 128]),
                            wload(wz_d[:, g], [8, 512]))

                wnext = s_loadw(0)
                for g in range(4):
                    ar_reset(off_s)
                    wx, wbc, wz = wnext
                    hs = slice(8 * g, 8 * g + 8)
                    upad = [ar([515], BF16) for _ in range(2)]
                    dg = ar([6, 4, 128], BF16)
                    TT(dg.ap, ident_b.ap[:].unsqueeze(1).unsqueeze(1).broadcast_to([128, 6, 4, 128]),
                       cw[:, 6 * g:6 * g + 6, :].unsqueeze(3).broadcast_to([128, 6, 4, 128]), ALU.mult, [ident_b.t, vecs.t], dg.t)
                    tiles = [(j, tt) for j in range(6) for tt in range(2)]

                    def p1_mm(n):
                        j, tt = tiles[n]
                        i = n % 2
                        wsrc = (wx, j) if j < 4 else (wbc, j - 4)
                        cols = slice(tt * 512, (tt + 1) * 512)
                        for kc in range(8):
                            MM(PB[i], wsrc[0].ap[:, wsrc[1], kc, :], hT.ap[:, kc, cols], kc == 0, kc == 7,
                               [wsrc[0].t, hT.t[tt]], PT[i])

                    def p1_rest(n):
                        j, tt = tiles[n]
                        i = n % 2
                        cc = 6 * g + j
                        cols = slice(tt * 512, (tt + 1) * 512)
                        up = upad[i]
                        CP(up.ap[:, 0:3], tails.ap[:, cc, :], tails.t[cc], up.t, raw=True)
                        CP(up.ap[:, 3:515], PB[i], PT[i], up.t, eng="act")
                        CP(tails.ap[:, cc, :], up.ap[:, 512:515], up.t, tails.t[cc], raw=True)
                        for k in range(4):
                            MM(PB[2 + i], dg.ap[:, j, k, :], up.ap[:, k:k + 512], k == 0, k == 3, [dg.t, up.t], PT[2 + i])
                        if j < 4:
                            dst, dt_tok = xsT.ap[:, j, cols], xsT.t
                        elif j == 4:
                            dst, dt_tok = BT.ap[:, cols], BT.t
                        else:
                            dst, dt_tok = CT.ap[:, cols], CT.t
                        ACT(dst, PB[2 + i], AF.Silu, [PT[2 + i], vecs.t], dt_tok, bias=cb[:, cc:cc + 1])

                    p1_mm(0)
                    for n in range(12):
                        if n + 1 < 12:
                            p1_mm(n + 1)
                        p1_rest(n)
                    if hf == 0 and g == 0:
                        DUMP("xsT", xsT.ap[:, 0, 0:128], xsT.t)
                    for c in range(8):
                        i = c % 2
                        cols = slice(c * 128, (c + 1) * 128)
                        for kc in range(8):
                            MM(PB[i], hT.ap[:, kc, cols], wz.ap[:, kc, :], kc == 0, kc == 7, [hT.t[c // 4], wz.t], PT[i])
                        ACT(sz_all.ap[:, c, :], PB[i], AF.Silu, PT[i], gtok(sz_all, c * 1024, (c + 1) * 1024))
                    if g + 1 < 4:
                        wnext = s_loadw(g + 1)
                    ar_reset(off_s)
                    xsD = [ar([8, 64], BF16) for _ in range(2)]
                    xdt = [ar([8, 64], BF16) for _ in range(2)]
                    xdte = [ar([8, 64], BF16) for _ in range(2)]
                    B_tok = [ar([128], BF16) for _ in range(2)]
                    cbTm = [ar([128], BF16) for _ in range(2)]
                    MT = [ar([8, 128], BF16) for _ in range(2)]
                    rhsS1 = ar([8, 128], F32)
                    rhsS = [rhsS1, rhsS1]
                    E = ar([8, 128], BF16)
                    y = ar([8, 64], F32)
                    sq = ar([512], F32)
                    yn = [ar([512], BF16) for _ in range(2)]

                    def s_front(c):
                        k = c % 2
                        cols = slice(c * 128, (c + 1) * 128)
                        TT(rhsS[k].ap, tri_f.ap[:].unsqueeze(1).broadcast_to([128, 8, 128]),
                           dA.ap[:, c, hs].unsqueeze(2).broadcast_to([128, 8, 128]), ALU.mult, [tri_f.t, dA.t], rhsS[k].t, eng="pool")
                        for i in range(4):
                            TR(PBb[2][:, i * 128:(i + 1) * 128], xsT.ap[:, i, cols], ident_b.ap[:], [xsT.t, ident_b.t], PT[2])
                        TR(PBb[2][:, 512:640], BT.ap[:, cols], ident_b.ap[:], [BT.t, ident_b.t], PT[2])
                        TT(xsD[k].ap, v8(PBb[2][:, 0:512]), D_bc[:, hs].unsqueeze(2).broadcast_to([128, 8, 64]), ALU.mult,
                           [PT[2], rows.t], xsD[k].t)
                        TT(xdt[k].ap, v8(PBb[2][:, 0:512]), dt_.ap[:, c, hs].unsqueeze(2).broadcast_to([128, 8, 64]), ALU.mult,
                           [PT[2], dt_.t], xdt[k].t)
                        TT(xdte[k].ap, v8(PBb[2][:, 0:512]), dtdte.ap[:, c, hs].unsqueeze(2).broadcast_to([128, 8, 64]), ALU.mult,
                           [PT[2], dtdte.t], xdte[k].t)
                        CP(B_tok[k].ap, PBb[2][:, 512:640], PT[2], B_tok[k].t, raw=True)
                        MM(PB[3][:, 0:128], BT.ap[:, cols], CT.ap[:, cols], True, True, [BT.t, CT.t], PT[3])
                        TT(cbTm[k].ap, PB[3][:, 0:128], tri_f.ap[:], ALU.mult, [PT[3], tri_f.t], cbTm[k].t)
                        for q in range(2):
                            MM(PB[4 + q], ugt_f.ap[:], rhsS[k].ap[:, 4 * q:4 * q + 4, :].rearrange("p a b -> p (a b)"), True, True,
                               [ugt_f.t, rhsS[k].t], PT[4 + q])
                            ACT(E.ap[:, 4 * q:4 * q + 4, :].rearrange("p a b -> p (a b)"), PB[4 + q], AF.Exp, PT[4 + q], E.t)
                        TT(MT[k].ap, E.ap, cbTm[k].ap.unsqueeze(1).broadcast_to([128, 8, 128]), ALU.mult, [E.t, cbTm[k].t], MT[k].t)

                    def s_tail(c):
                        k = c % 2
                        cols = slice(c * 128, (c + 1) * 128)
                        MM(PB[6], ident_b.ap[:], xsD[k].ap.rearrange("p a b -> p (a b)"), True, False, [ident_b.t, xsD[k].t], PT[6])
                        for h in range(8):
                            MM(PB[6][:, h * 64:(h + 1) * 64], MT[k].ap[:, h, :], xdt[k].ap[:, h, :], False, h == 7,
                               [MT[k].t, xdt[k].t], PT[6])
                        MM(PB[7], CT.ap[:, cols], Hin.ap[:, g, :], True, True, [CT.t, Hin.t[g]], PT[7])
                        TT(y.ap, v8(PB[7]), eacs.ap[:, c, hs].unsqueeze(2).broadcast_to([128, 8, 64]), ALU.mult, [PT[7], eacs.t], y.t)
                        TT(y.ap, y.ap, v8(PB[6]), ALU.add, [y.t, PT[6]], y.t)
                        yf = y.ap.rearrange("p a b -> p (a b)")
                        ygt = gtok(yg_all, c * 1024, (c + 1) * 1024)
                        TT(yg_all.ap[:, c, :], yf, sz_all.ap[:, c, :], ALU.mult, [y.t, gtok(sz_all, c * 1024, (c + 1) * 1024)], ygt)
                        ACT(sq.ap, yg_all.ap[:, c, :], AF.Square, ygt, [sq.t, ssg_all.t], accum=ssg_all.ap[:, c:c + 1])
                        MM(PB[0], B_tok[k].ap, xdte[k].ap.rearrange("p a b -> p (a b)"), True, True, [B_tok[k].t, xdte[k].t], PT[0])
                        TT(v8(Sst.ap[:, g, :]), v8(Sst.ap[:, g, :]), cd.ap[:, c, hs].unsqueeze(2).broadcast_to([128, 8, 64]), ALU.mult,
                           [Sst.t[g], cd.t], Sst.t[g], eng="pool")
                        TT(Sst.ap[:, g, :], Sst.ap[:, g, :], PB[0], ALU.add, [Sst.t[g], PT[0]], Sst.t[g])
                        CP(Hin.ap[:, g, :], Sst.ap[:, g, :], Sst.t[g], Hin.t[g], eng="act")

                    s_front(0)
                    for c in range(8):
                        if c + 1 < 8:
                            s_front(c + 1)
                        s_tail(c)
                    ACT(rsg_all.ap, ssg_all.ap, AF.Ln, [ssg_all.t, epsb.t], rsg_all.t, bias=epsb.ap[:], scale=1.0 / 512)
                    ACT(rsg_all.ap, rsg_all.ap, AF.Exp, rsg_all.t, rsg_all.t, scale=-0.5)

                    def p3_scale(c):
                        TS(yn[c % 2].ap, yg_all.ap[:, c, :], rsg_all.ap[:, c:c + 1], None, ALU.mult, None,
                           [gtok(yg_all, c * 1024, (c + 1) * 1024), rsg_all.t], yn[c % 2].t)

                    p3_scale(0)
                    for c in range(8):
                        k = c % 2
                        bk = 1 + 2 * k
                        cols = slice(c * 128, (c + 1) * 128)
                        if c + 1 < 8:
                            p3_scale(c + 1)
                        for i in range(4):
                            TR(PBb[bk][:, i * 128:(i + 1) * 128], yn[k].ap[:, i * 128:(i + 1) * 128], ident_b.ap[:],
                               [yn[k].t, ident_b.t], PT[bk])
                        TT(big.ap[:, 4 * g:4 * g + 4, cols], PBb[bk][:, 0:512].rearrange("p (a b) -> p a b", a=4, b=128),
                           nw[:, 4 * g:4 * g + 4].unsqueeze(2).broadcast_to([128, 4, 128]), ALU.mult, [PT[bk], vecs.t],
                           [yT_t(4 * g + i, c // 4) for i in range(4)])
                if hf == 0:
                    DUMP("yT", big.ap[:, 0:16, 0:128], big.t)
                if upto == "S":
                    continue

                ar_reset()
                qT = [ar([TH], BF16) for _ in range(2)]
                if hf == 1:
                    kcur = [ar([TH], BF16) for _ in range(2)]
                    vcur = [ar([8, 130], BF16) for _ in range(2)]
                    for i in range(2):
                        MSET(vcur[i].ap, 1.0, vcur[i].t)
                qkf = ar([8, 256], F32)
                qkb = ar([8, 256], BF16)
                rt = [ar([8, 4, 8], F32) for _ in range(2)]
                Pm = [ar([512], BF16) for _ in range(4)]
                Oacc = ar([8, 2, 130], F32)
                rec = ar([8, 2], F32)
                nl2 = ar([8], F32)
                o1 = ar([8, 128], F32)
                o2 = ar([8, 128], F32)
                sso = ar([8], F32)
                rso = ar([8], F32)
                on = [ar([8, 128], BF16) for _ in range(2)]

                def gtoks(buf, lo, hi):
                    return buf.t[lo // (2 * GRAN):(hi + 2 * GRAN - 1) // (2 * GRAN)]

                def a_inproj(h):
                    hp = h % 2
                    wq = wload(wqkv_d[:, h], [8, 384])
                    for b in range(8):
                        i = 2 * (b % 2)
                        bcols = slice(b * 128, (b + 1) * 128)
                        for kc in range(8):
                            MM(PB[i][:, 0:384], hT.ap[:, kc, bcols], wq.ap[:, kc, :], kc == 0, kc == 7, [hT.t[b // 4], wq.t], PT[i])
                        TS(qkf.ap[:, b, :], PB[i][:, 0:256], 1.0, None, ALU.mult, None, PT[i], gtoks(qkf, b * 1024, (b + 1) * 1024))
                        if hf == 0:
                            vdst, vtok = vcache.ap[:, h, b, 0:128], vcache.t[h * 8 + b]
                        else:
                            vdst, vtok = vcur[hp].ap[:, b, 0:128], vcur[hp].t
                        TS(vdst, PB[i][:, 256:384], 1.0, None, ALU.mult, None, PT[i], vtok)

                def a_rope(h):
                    gb0 = hf * 8
                    s5 = qkf.ap.rearrange("p b (s d) -> p b s d", s=4, d=64)
                    d5 = qkb.ap.rearrange("p b (s d) -> p b s d", s=4, d=64)
                    cosb = sc.ap[:, gb0:gb0 + 8, 8:16].unsqueeze(2).broadcast_to([128, 8, 4, 8])
                    sinb = sc.ap[:, gb0:gb0 + 8, 0:8].unsqueeze(2).broadcast_to([128, 8, 4, 8])
                    x1 = s5[:, :, :, 0:8]
                    x2 = s5[:, :, :, 8:16]
                    return [
                        lambda: TS(qkb.ap.rearrange("p b c -> p (b c)"), qkf.ap.rearrange("p b c -> p (b c)"), 1.0, None, ALU.mult, None, qkf.t, qkb.t),
                        lambda: TT(rt[0].ap, x1, cosb, ALU.mult, [qkf.t, sc.t], rt[0].t),
                        lambda: TT(rt[1].ap, x2, sinb, ALU.mult, [qkf.t, sc.t], rt[1].t),
                        lambda: TT(d5[:, :, :, 0:8], rt[0].ap, rt[1].ap, ALU.subtract, [rt[0].t, rt[1].t], qkb.t),
                        lambda: TT(rt[0].ap, x2, cosb, ALU.mult, [qkf.t, sc.t], rt[0].t),
                        lambda: TT(rt[1].ap, x1, sinb, ALU.mult, [qkf.t, sc.t], rt[1].t),
                        lambda: TT(d5[:, :, :, 8:16], rt[0].ap, rt[1].ap, ALU.add, [rt[0].t, rt[1].t], qkb.t),
                    ]

                def a_transposes(h):
                    hp = h % 2
                    pv4 = PBb[2].rearrange("p (j t c) -> p j t c", j=4, t=2, c=128)
                    for h4 in range(2):
                        for j in range(4):
                            b = h4 * 4 + j
                            TR(PBb[2][:, j * 256:j * 256 + 128], qkb.ap[:, b, 0:128], ident_b.ap[:], [qkb.t, ident_b.t], PT[2])
                            TR(PBb[2][:, j * 256 + 128:j * 256 + 256], qkb.ap[:, b, 128:256], ident_b.ap[:], [qkb.t, ident_b.t], PT[2])
                        c4 = slice(h4 * 512, (h4 + 1) * 512)
                        CP(qT[hp].ap[:, c4].rearrange("p (j c) -> p j c", j=4, c=128), pv4[:, :, 0, :], PT[2], qT[hp].t, raw=True)
                        if hf == 0:
                            kd, kt = kcache.ap[:, h, c4], [kcache.t[h * 8 + h4 * 4 + j] for j in range(4)]
                        else:
                            kd, kt = kcur[hp].ap[:, c4], kcur[hp].t
                        CP(kd.rearrange("p (j c) -> p j c", j=4, c=128), pv4[:, :, 1, :], PT[2], kt, raw=True)
                    if hf == 0 and h == 0:
                        DUMP("qT", qT[0].ap[:, 0:128], qT[0].t)

                nsc = [0]

                def a_attn(h, qbs):
                    hp = h % 2

                    def ksrc(kb):
                        if kb < 8:
                            return kcache.ap[:, h, kb * 128:(kb + 1) * 128], kcache.t[h * 8 + kb]
                        return kcur[hp].ap[:, (kb - 8) * 128:(kb - 7) * 128], kcur[hp].t

                    def vsrc(kb):
                        if kb < 8:
                            return vcache.ap[:, h, kb, 0:129], vcache.t[h * 8 + kb]
                        return vcur[hp].ap[:, kb - 8, 0:129], vcur[hp].t

                    groups = []
                    for qb in qbs:
                        gq = hf * 8 + qb
                        nkb = gq + 1
                        ng = (nkb + 3) // 4
                        for kg in range(ng):
                            kbs = list(range(4 * kg, min(4 * kg + 4, nkb)))
                            groups.append((qb, gq, nkb, kbs, kg == ng - 1))
                    SB = ((3, 4), (5, 1))

                    def s1(gi):
                        qb, gq, nkb, kbs, last = groups[gi]
                        banks = SB[(nsc[0] + gi) % 2]
                        qcols = slice(qb * 128, (qb + 1) * 128)
                        for jj, kb in enumerate(kbs):
                            ka, kt = ksrc(kb)
                            for m in range(2):
                                pr = slice(64 * m, 64 * m + 64)
                                MM(PB[banks[m]][:, jj * 128:(jj + 1) * 128], ka[pr, :], qT[hp].ap[pr, qcols], True, True,
                                   [kt, qT[hp].t], PT[banks[m]])

                    def s23(gi):
                        qb, gq, nkb, kbs, last = groups[gi]
                        banks = SB[(nsc[0] + gi) % 2]
                        w_ = len(kbs) * 128
                        for m in range(2):
                            pm = Pm[2 * ((nsc[0] + gi) % 2) + m]
                            ACT(pm.ap[:, 0:w_], PB[banks[m]][:, 0:w_], AF.Exp, PT[banks[m]], pm.t, scale=0.125)
                            if gq in kbs:
                                jj = kbs.index(gq)
                                TT(pm.ap[:, jj * 128:(jj + 1) * 128], pm.ap[:, jj * 128:(jj + 1) * 128], tri_b.ap[:], ALU.mult,
                                   [pm.t, tri_b.t], pm.t)
                            for jj, kb in enumerate(kbs):
                                va, vt = vsrc(kb)
                                MM(PB[6 + m][:, 0:129], pm.ap[:, jj * 128:(jj + 1) * 128], va, kb == 0, kb == nkb - 1,
                                   [pm.t, vt], PT[6 + m])
                        if last:
                            ot_ = gtoks(Oacc, qb * 1040, (qb + 1) * 1040)
                            TS(Oacc.ap[:, qb, 0, 0:129], PB[6][:, 0:129], 1.0, None, ALU.mult, None, PT[6], ot_)
                            TS(Oacc.ap[:, qb, 1, 0:129], PB[7][:, 0:129], 1.0, None, ALU.mult, None, PT[7], ot_)

                    s1(0)
                    for gi in range(len(groups)):
                        if gi + 1 < len(groups):
                            s1(gi + 1)
                        s23(gi)
                        for _ in range(2):
                            if pend:
                                pend.pop(0)()
                    nsc[0] += len(groups)

                def a_epi_dve(h):
                    hp = h % 2
                    b8 = lambda ap: ap.unsqueeze(2).broadcast_to([128, 8, 128])
                    RCP(rec.ap, Oacc.ap[:, :, :, 128], Oacc.t, rec.t)
                    TS(nl2.ap, rec.ap[:, :, 1], neglam.ap[:], None, ALU.mult, None, [rec.t, neglam.t], nl2.t)
                    TT(o1.ap, Oacc.ap[:, :, 0, 0:128], b8(rec.ap[:, :, 0]), ALU.mult, [Oacc.t, rec.t], o1.t)
                    TT(o2.ap, Oacc.ap[:, :, 1, 0:128], b8(nl2.ap), ALU.mult, [Oacc.t, nl2.t], o2.t)
                    return [
                        lambda: TT(o1.ap, o1.ap, o2.ap, ALU.add, [o1.t, o2.t], o1.t),
                        lambda: TT(o2.ap, o1.ap, o1.ap, ALU.mult, o1.t, o2.t),
                        lambda: TRED(sso.ap, o2.ap, o2.t, sso.t),
                        lambda: ACT(rso.ap, sso.ap, AF.Ln, [sso.t, epsb.t], rso.t, bias=epsb.ap[:], scale=1.0 / 128),
                        lambda: ACT(rso.ap, rso.ap, AF.Exp, rso.t, rso.t, scale=-0.5),
                        lambda: TT(on[hp].ap, o1.ap, b8(rso.ap), ALU.mult, [o1.t, rso.t], on[hp].t),
                    ]

                def a_epi_tr(h):
                    hp = h % 2
                    for h4 in range(2):
                        for j in range(4):
                            TR(PBb[2][:, j * 128:(j + 1) * 128], on[hp].ap[:, h4 * 4 + j, :], ident_b.ap[:], [on[hp].t, ident_b.t], PT[2])
                        TS(big.ap[:, 16 + h, h4 * 512:(h4 + 1) * 512], PBb[2][:, 0:512], swl.ap[:], None, ALU.mult, None,
                           [PT[2], swl.t], yT_t(16 + h, h4))

                pend = []

                def flush():
                    while pend:
                        pend.pop(0)()

                a_inproj(0)
                pend.extend(a_rope(0))
                flush()
                a_transposes(0)
                for h in range(8):
                    if h + 1 < 8:
                        a_inproj(h + 1)
                        pend.extend(a_rope(h + 1))
                    a_attn(h, range(0, 8))
                    flush()
                    if h > 0:
                        a_epi_tr(h - 1)
                    if h + 1 < 8:
                        a_transposes(h + 1)
                    pend.extend(a_epi_dve(h))
                flush()
                a_epi_tr(7)
                if hf == 0:
                    DUMP("oT", big.ap[:, 16:24, 0:128], big.t)
                if upto == "A":
                    continue

                ar_reset()
                mixT = ar([8, TH], BF16)
                s1 = ar([512], F32)
                s2 = ar([512], F32)
                n = 0
                for c in range(8):
                    wc = wload(wc_d[:, c], [40, 128])
                    for tt in range(2):
                        cols = slice(tt * 512, (tt + 1) * 512)
                        b0 = 4 * (n % 2)
                        n += 1
                        for kc in range(16):
                            MM(PB[b0], wc.ap[:, kc, :], big.ap[:, kc, cols], kc == 0, kc == 15, [wc.t, yT_t(kc, tt)], PT[b0])
                        for kc in range(8):
                            MM(PB[b0 + 1], wc.ap[:, 16 + kc, :], big.ap[:, 16 + kc, cols], kc == 0, kc == 7, [wc.t, yT_t(16 + kc, tt)],
                               PT[b0 + 1])
                        for kc in range(8):
                            MM(PB[b0 + 2], wc.ap[:, 24 + kc, :], hT.ap[:, kc, cols], kc == 0, kc == 7, [wc.t, hT.t[tt]], PT[b0 + 2])
                        for kc in range(8):
                            MM(PB[b0 + 3], wc.ap[:, 32 + kc, :], hT.ap[:, kc, cols], kc == 0, kc == 7, [wc.t, hT.t[tt]], PT[b0 + 3])
                        ACT(s1.ap, PB[b0 + 2], AF.Sigmoid, PT[b0 + 2], s1.t)
                        ACT(s2.ap, PB[b0 + 3], AF.Sigmoid, PT[b0 + 3], s2.t)
                        TT(s1.ap, s1.ap, PB[b0], ALU.mult, [s1.t, PT[b0]], s1.t)
                        TT(s2.ap, s2.ap, PB[b0 + 1], ALU.mult, [s2.t, PT[b0 + 1]], s2.t)
                        TT(mixT.ap[:, c, cols], s1.ap, s2.ap, ALU.add, [s1.t, s2.t], mixT.t)
                if hf == 0:
                    DUMP("mixT", mixT.ap[:, :, 0:128], mixT.t)
                if upto == "C":
                    continue

                g2bc = ar([D], F32)
                P.dma("sp", g2bc.ap, g24_d[0].partition_broadcast(128), (), g2bc.t)
                xb = [ar([D], F32) for _ in range(2)]
                x2t = [ar([D], F32) for _ in range(2)]
                xs = [ar([D], BF16) for _ in range(2)]
                junk = ar([D], F32)
                ss = [ar([1], F32) for _ in range(2)]
                rstd = [ar([1], F32) for _ in range(2)]
                wm = [wload(wmix_d[:, nh], [8, 512]) for nh in range(2)]
                out_tok = [P.tok(f"out{b}") for b in range(8)]
                def m_mm(b):
                    bk = 2 * (b % 2)
                    bcols = slice(b * 128, (b + 1) * 128)
                    for nh in range(2):
                        for kc in range(8):
                            MM(PB[bk + nh], mixT.ap[:, kc, bcols], wm[nh].ap[:, kc, :], kc == 0, kc == 7, [mixT.t, wm[nh].t], PT[bk + nh])

                def m_chain(b):
                    i = b % 2
                    bk = 2 * i
                    rowsl = slice(t0 + b * 128, t0 + (b + 1) * 128)
                    pm2 = ps[:, bk:bk + 2, :].rearrange("p a b -> p (a b)")
                    ptk = [PT[bk], PT[bk + 1]]
                    P.dma("sp", xb[i].ap, x_d[rowsl, :], (), xb[i].t)
                    rms_rstd(pm2, D, ptk, junk, ss[i], rstd[i])
                    STT(x2t[i].ap, pm2, rstd[i].ap, g2bc.ap, ALU.mult, ALU.mult, [ptk, rstd[i].t, g2bc.t], x2t[i].t)
                    TT(x2t[i].ap, x2t[i].ap, xb[i].ap, ALU.add, [x2t[i].t, xb[i].t], x2t[i].t)
                    P.dma("act", out_d[rowsl, :], x2t[i].ap, x2t[i].t, out_tok[b])
                    rms_rstd(x2t[i].ap, D, x2t[i].t, junk, ss[i], rstd[i])
                    TS(xs[i].ap, x2t[i].ap, rstd[i].ap, None, ALU.mult, None, [x2t[i].t, rstd[i].t], xs[i].t)

                m_mm(0)
                for b in range(8):
                    if b + 1 < 8:
                        m_mm(b + 1)
                    m_chain(b)
                    to_hT(xs[b % 2], b, g3, 4 + b % 2)
                if upto == "M":
                    continue

                ar_reset()
                fT = ar([8, 512], F32)
                g4bc = ar([D], F32)
                P.dma("sp", g4bc.ap, g24_d[1].partition_broadcast(128), (), g4bc.t)
                sg = [ar([512], F32) for _ in range(2)]
                x2b = [ar([D], F32) for _ in range(2)]
                ot = [ar([D], F32) for _ in range(2)]
                junk = ar([D], F32)
                ss = [ar([1], F32) for _ in range(2)]
                rstd = [ar([1], F32) for _ in range(2)]
                n = 0
                for hg in range(NHC // 2):
                    wg = wload(wgu_d[:, 2 * hg:2 * hg + 2], [2, 16, 128])
                    for j in range(2):
                        hc = 2 * hg + j
                        for tt in range(2):
                            cols = slice(tt * 512, (tt + 1) * 512)
                            i = n % 2
                            bk = 2 * i
                            n += 1
                            for kc in range(8):
                                MM(PB[bk], wg.ap[:, j, kc, :], hT.ap[:, kc, cols], kc == 0, kc == 7, [wg.t, hT.t[tt]], PT[bk])
                            for kc in range(8):
                                MM(PB[bk + 1], wg.ap[:, j, 8 + kc, :], hT.ap[:, kc, cols], kc == 0, kc == 7, [wg.t, hT.t[tt]], PT[bk + 1])
                            ACT(sg[i].ap, PB[bk], AF.Silu, PT[bk], sg[i].t)
                            TT(big.ap[:, hc, cols], sg[i].ap, PB[bk + 1], ALU.mult, [sg[i].t, PT[bk + 1]], yT_t(hc, tt))
                for tt in range(2):
                    cols = slice(tt * 512, (tt + 1) * 512)
                    for c in range(8):
                        wdn = wload(wd_d[:, c], [NHC, 128])
                        bk = 4 + c % 2
                        for hc in range(NHC):
                            MM(PB[bk], wdn.ap[:, hc, :], big.ap[:, hc, cols], hc == 0, hc == NHC - 1, [wdn.t, yT_t(hc, tt)], PT[bk])
                        CP(fT.ap[:, c, :], PB[bk], PT[bk], fT.t, eng="act")
                    def f_tr(b4):
                        pb0 = 6 if b4 % 2 == 0 else 0
                        for c in range(8):
                            TR(PB[pb0 + c // 4][:, (c % 4) * 128:(c % 4 + 1) * 128], fT.ap[:, c, b4 * 128:(b4 + 1) * 128], ident_f.ap[:],
                               [fT.t, ident_f.t], PT[pb0 + c // 4])

                    def f_chain(b4):
                        b = tt * 4 + b4
                        i = b % 2
                        pb0 = 6 if b4 % 2 == 0 else 0
                        rowsl = slice(t0 + b * 128, t0 + (b + 1) * 128)
                        pf = ps[:, pb0:pb0 + 2, :].rearrange("p a b -> p (a b)")
                        ptk = [PT[pb0], PT[pb0 + 1]]
                        P.dma("sp", x2b[i].ap, out_d[rowsl, :], out_tok[b], x2b[i].t)
                        rms_rstd(pf, D, ptk, junk, ss[i], rstd[i])
                        STT(ot[i].ap, pf, rstd[i].ap, g4bc.ap, ALU.mult, ALU.mult, [ptk, rstd[i].t, g4bc.t], ot[i].t)
                        TT(ot[i].ap, ot[i].ap, x2b[i].ap, ALU.add, [ot[i].t, x2b[i].t], ot[i].t)
                        P.dma("act", out_d[rowsl, :], ot[i].ap, ot[i].t, out_tok[b])

                    f_tr(0)
                    for b4 in range(4):
                        if b4 + 1 < 4:
                            f_tr(b4 + 1)
                        f_chain(b4)

        except _Stop:
            pass
        P.emit()
        build.stats = (P.stats, P.nwaits)
    return nc


def _tile_w(W, ncol):
    K, N = W.shape
    return np.ascontiguousarray(W.reshape(K // 128, 128, N // ncol, ncol).transpose(1, 2, 0, 3))


def _prep_shared(inp):
    w_in = np.asarray(inp["w_in"][0], dtype=np.float32)
    sh = {}
    xbc_t = _tile_w(w_in[:, X0:DT0], 128)
    order = []
    for g in range(4):
        order += [4 * g, 4 * g + 1, 4 * g + 2, 4 * g + 3, 16 + g, 20 + g]
    sh["w_xbc"] = np.ascontiguousarray(xbc_t[:, order])
    sh["w_z"] = _tile_w(w_in[:, Z0:X0], 512)
    sh["w_dt"] = np.ascontiguousarray(_tile_w(w_in[:, DT0:Q0], 32)[:, 0])
    qkv = np.concatenate([w_in[:, Q0:K0].reshape(D, 8, 128), w_in[:, K0:V0].reshape(D, 8, 128),
                          w_in[:, V0:G0].reshape(D, 8, 128)], axis=2).reshape(D, 8 * 384)
    sh["w_qkv"] = _tile_w(qkv, 384)
    wso = _tile_w(np.asarray(inp["w_ssm_out"][0], np.float32), 128)
    wao = _tile_w(np.asarray(inp["w_attn_out"][0], np.float32), 128)
    wgs = _tile_w(w_in[:, G0:G0 + D], 128)
    wga = _tile_w(w_in[:, G0 + D:G0 + 2 * D], 128)
    sh["w_c"] = np.ascontiguousarray(np.concatenate([wso, wao, wgs, wga], axis=2))
    sh["w_mix"] = _tile_w(np.asarray(inp["w_mix_out"][0], np.float32), 512)
    wg = _tile_w(np.asarray(inp["w_ffn_gate"][0], np.float32), 128)
    wu = _tile_w(np.asarray(inp["w_ffn_up"][0], np.float32), 128)
    sh["w_gu"] = np.ascontiguousarray(np.concatenate([wg, wu], axis=2))
    sh["w_d"] = _tile_w(np.asarray(inp["w_ffn_down"][0], np.float32), 128)

    def pc(v, n):
        return np.asarray(v, np.float32).reshape(n, 128).T

    conv_w = np.asarray(inp["conv_w"][0], np.float32)
    cwt = conv_w.T.reshape(24, 128, 4).transpose(1, 0, 2)[:, order]
    cbt = pc(inp["conv_b"][0], 24)[:, order]
    sh["vecs"] = np.ascontiguousarray(np.concatenate([
        pc(inp["norm_pre_mix"][0], 8), pc(inp["norm_pre_ffn"][0], 8), pc(inp["ssm_norm_w"][0], 16), cbt,
        cwt.reshape(128, 96), np.asarray(inp["attn_subln_w"][0], np.float32).reshape(128, 1)], axis=1))
    sh["rows"] = np.ascontiguousarray(np.concatenate([
        np.asarray(inp[k][0], np.float32) for k in ("dt_bias", "a_log", "d_skip", "lam_q1", "lam_k1", "lam_q2", "lam_k2")]))
    sh["g24"] = np.ascontiguousarray(np.stack([np.asarray(inp["norm_post_mix"][0], np.float32),
                                               np.asarray(inp["norm_post_ffn"][0], np.float32)]))
    return sh


def _in_maps(inp, cores):
    sh = _prep_shared(inp)
    maps = []
    for c in cores:
        m = dict(sh)
        m["x"] = np.ascontiguousarray(np.asarray(inp["x"][c], np.float32))
        m["posr"] = np.ascontiguousarray(np.asarray(inp["positions"][c], np.int32).reshape(16, 128).T)
        maps.append(m)
    return maps


_NC = None


def kernel(**inputs):
    global _NC
    if _NC is None:
        _NC = build()
    maps = _in_maps(inputs, list(range(8)))
    res = run_bass_kernel_spmd(_NC, maps, core_ids=list(range(8)))
    return np.stack([np.asarray(r["out"], np.float32) for r in res.results], axis=0)
```
